# Optimizing a Trainium2 kernel written in Bass

```python
import math
import jax, jax.numpy as jnp
from jax import lax
import numpy as np

D_MODEL = 2048
BATCH = 4
SEQ = 2048
DEPTH = 2
DEC_BATCH = 128
DEC_SEQ = 4
PAST_LEN = 16384
PAGE_SIZE = 128

D_RET = D_MODEL // 2
RET_HEADS = 4
RET_HEAD_DIM = D_RET // RET_HEADS
D_POOL = D_MODEL - D_RET
POOL_WINDOWS = (2, 4, 8, 16)
N_POOL_GROUPS = len(POOL_WINDOWS)
POOL_GROUP_DIM = D_POOL // N_POOL_GROUPS
POOL_BUF = max(POOL_WINDOWS) - 1
D_IN = 4 * D_RET + D_POOL
N_MEM = 256
MEM_HEADS = 4
MEM_HEAD_DIM = D_MODEL // MEM_HEADS
D_FF = 4 * D_MODEL
RET_CHUNK = 128
ROPE_BASE = 10000.0
EPS = 1e-6

kernel_name = 'hymba_retention_pool_memory_decode_step'


def rmsnorm(x, w):
    xf = x.astype(jnp.float32)
    y = xf * lax.rsqrt(jnp.mean(xf * xf, axis=-1, keepdims=True) + EPS)
    return (y * w.astype(jnp.float32)).astype(x.dtype)


def rotary(x, pos):
    half = x.shape[-1] // 2
    inv = ROPE_BASE ** (-jnp.arange(half, dtype=jnp.float32) / half)
    ang = pos.astype(jnp.float32)[:, None] * inv[None, :]
    cos = jnp.cos(ang)[None, :, None, :]
    sin = jnp.sin(ang)[None, :, None, :]
    xf = x.astype(jnp.float32)
    x1, x2 = xf[..., :half], xf[..., half:]
    return jnp.concatenate([x1 * cos - x2 * sin, x2 * cos + x1 * sin], axis=-1)


def log_gamma():
    return jnp.log1p(-jnp.exp2(-5.0 - jnp.arange(RET_HEADS, dtype=jnp.float32)))


def retention(q, k, v, s0, chunk):
    B, T, H, Dk = q.shape
    Dv = v.shape[-1]
    nc = T // chunk
    lg = log_gamma()
    idx = jnp.arange(chunk, dtype=jnp.float32)
    diff = idx[:, None] - idx[None, :]
    decay_in = jnp.where(diff[None] >= 0.0,
                         jnp.exp(lg[:, None, None] * jnp.maximum(diff, 0.0)[None]), 0.0)
    decay_q = jnp.exp(lg[None, :] * (idx[:, None] + 1.0))
    decay_k = jnp.exp(lg[None, :] * (chunk - 1.0 - idx[:, None]))
    decay_chunk = jnp.exp(lg * chunk)

    def split(a):
        return a.reshape(B, nc, chunk, H, a.shape[-1]).swapaxes(0, 1)

    def step(s, qkv):
        qc, kc, vc = qkv
        scores = jnp.einsum('bihd,bjhd->bhij', qc, kc) * decay_in[None]
        o = jnp.einsum('bhij,bjhe->bihe', scores, vc)
        o = o + jnp.einsum('bihd,bhde->bihe', qc * decay_q[None, :, :, None], s)
        s = s * decay_chunk[None, :, None, None] + jnp.einsum(
            'bjhd,bjhe->bhde', kc * decay_k[None, :, :, None], vc)
        return s, o

    s, o = lax.scan(step, s0, (split(q), split(k), split(v)))
    return o.swapaxes(0, 1).reshape(B, T, H, Dv), s


def pool_mix(u, buf, n_prev):
    T = u.shape[1]
    uf = u.astype(jnp.float32)
    ext = jnp.concatenate([buf.astype(jnp.float32), uf], axis=1)
    cs = jnp.pad(jnp.cumsum(ext, axis=1), ((0, 0), (1, 0), (0, 0)))
    end = cs[:, POOL_BUF + 1:]
    outs = []
    for g, w in enumerate(POOL_WINDOWS):
        sl = slice(g * POOL_GROUP_DIM, (g + 1) * POOL_GROUP_DIM)
        start = cs[:, POOL_BUF + 1 - w: POOL_BUF + 1 - w + T, sl]
        cnt = jnp.minimum(jnp.arange(T) + n_prev + 1, w).astype(jnp.float32)
        outs.append((end[..., sl] - start) / cnt[None, :, None] - uf[..., sl])
    pooled = jnp.stack(outs, axis=2)
    return pooled, ext[:, -POOL_BUF:].astype(u.dtype)


def mem_kv(mem, norm_w, w_k, w_v):
    B = mem.shape[0]
    m = rmsnorm(mem, norm_w)
    k = (m @ w_k).reshape(B, N_MEM, MEM_HEADS, MEM_HEAD_DIM)
    v = (m @ w_v).reshape(B, N_MEM, MEM_HEADS, MEM_HEAD_DIM)
    return k, v


def cross_attend(h, mk, mv, w_q):
    B, T, _ = h.shape
    q = (h @ w_q).reshape(B, T, MEM_HEADS, MEM_HEAD_DIM)
    s = jnp.einsum('bthd,bmhd->bhtm', q, mk).astype(jnp.float32) * (MEM_HEAD_DIM ** -0.5)
    p = jax.nn.softmax(s, axis=-1).astype(h.dtype)
    return jnp.einsum('bhtm,bmhd->bthd', p, mv).reshape(B, T, D_MODEL)


def layer(x, pos, s_ret, p_buf, n_prev, chunk, mk, mv, lw):
    B, T, _ = x.shape
    h = rmsnorm(x, lw['attn_norm_w'])
    z = h @ lw['w_in']
    q, k, v, g, u = jnp.split(z, [D_RET, 2 * D_RET, 3 * D_RET, 4 * D_RET], axis=-1)
    heads = lambda a: a.reshape(B, T, RET_HEADS, RET_HEAD_DIM)
    qr = rotary(heads(q), pos)
    kr = rotary(heads(k), pos) * (RET_HEAD_DIM ** -0.5)
    o, s_new = retention(qr, kr, heads(v).astype(jnp.float32), s_ret.astype(jnp.float32), chunk)
    o = o * lax.rsqrt(jnp.mean(o * o, axis=-1, keepdims=True) + EPS)
    o = o.reshape(B, T, D_RET) * lw['ret_norm_w'].astype(jnp.float32)
    o = (jax.nn.silu(g.astype(jnp.float32)) * o).astype(x.dtype)
    pooled, buf_new = pool_mix(u, p_buf, n_prev)
    pm = jnp.einsum('btgc,gcd->btgd', pooled.astype(x.dtype), lw['pool_w']).reshape(B, T, D_POOL)
    pm = pm * lw['pool_scale']
    x = x + jnp.concatenate([o, pm], axis=-1) @ lw['w_out']
    h = rmsnorm(x, lw['xattn_norm_w'])
    x = x + cross_attend(h, mk, mv, lw['w_xq']) @ lw['w_xo']
    h = rmsnorm(x, lw['mlp_norm_w'])
    x = x + jnp.square(jax.nn.relu(h @ lw['w_up'])) @ lw['w_down']
    return x, s_new.astype(x.dtype), buf_new


def setup_inputs(seed: int = 0) -> dict:
    key = jax.random.key(seed)
    ks = jax.random.split(key, 24)
    f32 = jnp.float32
    nrm = lambda k, shape, s: jax.random.normal(k, shape, f32) * s
    gain = lambda k, shape: 1.0 + 0.02 * jax.random.normal(k, shape, f32)
    return {
        'x_prompt': nrm(ks[0], (BATCH, SEQ, D_MODEL), 1.0),
        'x_sample': nrm(ks[1], (DEC_BATCH, DEC_SEQ, D_MODEL), 1.0),
        'mem_prompt': nrm(ks[2], (BATCH, N_MEM, D_MODEL), 1.0),
        'state_ret': nrm(ks[3], (DEPTH, DEC_BATCH, RET_HEADS, RET_HEAD_DIM, RET_HEAD_DIM), 0.5),
        'state_pool': nrm(ks[4], (DEPTH, DEC_BATCH, POOL_BUF, D_POOL), 1.0),
        'cache_mem_k': nrm(ks[5], (DEPTH, DEC_BATCH, N_MEM, MEM_HEADS, MEM_HEAD_DIM), 1.0),
        'cache_mem_v': nrm(ks[6], (DEPTH, DEC_BATCH, N_MEM, MEM_HEADS, MEM_HEAD_DIM), 1.0),
        'attn_norm_w': gain(ks[7], (DEPTH, D_MODEL)),
        'w_in': nrm(ks[8], (DEPTH, D_MODEL, D_IN), D_MODEL ** -0.5),
        'ret_norm_w': gain(ks[9], (DEPTH, D_RET)),
        'pool_w': nrm(ks[10], (DEPTH, N_POOL_GROUPS, POOL_GROUP_DIM, POOL_GROUP_DIM), POOL_GROUP_DIM ** -0.5),
        'pool_scale': gain(ks[11], (DEPTH, D_POOL)),
        'w_out': nrm(ks[12], (DEPTH, D_MODEL, D_MODEL), D_MODEL ** -0.5),
        'xattn_norm_w': gain(ks[13], (DEPTH, D_MODEL)),
        'mem_norm_w': gain(ks[14], (DEPTH, D_MODEL)),
        'w_xq': nrm(ks[15], (DEPTH, D_MODEL, D_MODEL), D_MODEL ** -0.5),
        'w_mk': nrm(ks[16], (DEPTH, D_MODEL, D_MODEL), D_MODEL ** -0.5),
        'w_mv': nrm(ks[17], (DEPTH, D_MODEL, D_MODEL), D_MODEL ** -0.5),
        'w_xo': nrm(ks[18], (DEPTH, D_MODEL, D_MODEL), D_MODEL ** -0.5),
        'mlp_norm_w': gain(ks[19], (DEPTH, D_MODEL)),
        'w_up': nrm(ks[20], (DEPTH, D_MODEL, D_FF), D_MODEL ** -0.5),
        'w_down': nrm(ks[21], (DEPTH, D_FF, D_MODEL), D_FF ** -0.5),
        'final_norm_w': gain(ks[22], (D_MODEL,)),
    }


def reference(x_prompt, x_sample, mem_prompt, state_ret, state_pool, cache_mem_k, cache_mem_v,
              attn_norm_w, w_in, ret_norm_w, pool_w, pool_scale, w_out, xattn_norm_w, mem_norm_w,
              w_xq, w_mk, w_mv, w_xo, mlp_norm_w, w_up, w_down, final_norm_w):
    pos_p = jnp.arange(SEQ)
    pos_s = jnp.arange(DEC_SEQ) + PAST_LEN
    xp, xs = x_prompt, x_sample
    ret_p, buf_p, mk_p, mv_p, ret_s, buf_s = [], [], [], [], [], []
    for l in range(DEPTH):
        lw = {'attn_norm_w': attn_norm_w[l], 'w_in': w_in[l], 'ret_norm_w': ret_norm_w[l],
              'pool_w': pool_w[l], 'pool_scale': pool_scale[l], 'w_out': w_out[l],
              'xattn_norm_w': xattn_norm_w[l], 'w_xq': w_xq[l], 'w_xo': w_xo[l],
              'mlp_norm_w': mlp_norm_w[l], 'w_up': w_up[l], 'w_down': w_down[l]}
        mk, mv = mem_kv(mem_prompt, mem_norm_w[l], w_mk[l], w_mv[l])
        s0 = jnp.zeros((BATCH, RET_HEADS, RET_HEAD_DIM, RET_HEAD_DIM), jnp.float32)
        b0 = jnp.zeros((BATCH, POOL_BUF, D_POOL), xp.dtype)
        xp, sp, bp = layer(xp, pos_p, s0, b0, 0, RET_CHUNK, mk, mv, lw)
        ret_p.append(sp)
        buf_p.append(bp)
        mk_p.append(mk)
        mv_p.append(mv)
        xs, ss, bs = layer(xs, pos_s, state_ret[l], state_pool[l], PAST_LEN, DEC_SEQ,
                           cache_mem_k[l], cache_mem_v[l], lw)
        ret_s.append(ss)
        buf_s.append(bs)
    y_prompt = rmsnorm(xp, final_norm_w)
    y_sample = rmsnorm(xs, final_norm_w)
    return (y_prompt, y_sample, jnp.stack(ret_p), jnp.stack(buf_p), jnp.stack(mk_p), jnp.stack(mv_p),
            jnp.stack(ret_s), jnp.stack(buf_s))
```

```python
import contextlib
import numpy as np
import ml_dtypes
import concourse.bass as bass
import concourse.mybir as mybir
from concourse.bass_utils import run_bass_kernel_spmd

F32 = mybir.dt.float32
BF16 = mybir.dt.bfloat16
ALU = mybir.AluOpType
AF = mybir.ActivationFunctionType
AX = mybir.AxisListType

D = 2048
NPT = 8
NT = NPT + 1
TOK = NPT * 128 + 64
SROW = NPT * 128
EPS = 1e-6
GAM = [1.0 - 2.0 ** (-5 - h) for h in range(4)]
WINS = (2, 4, 8, 16)


def trows(t):
    return 128 if t < NPT else 64


class Tr:
    def __init__(self, nc, es):
        self.nc = nc
        self.eng = {'pe': nc.tensor, 'act': nc.scalar, 'dve': nc.vector, 'pool': nc.gpsimd, 'sp': nc.sync}
        self.sem = {e: es.enter_context(nc.semaphore('s_' + e)) for e in self.eng}
        self.cnt = {e: 0 for e in self.eng}
        self.es = es
        self.dsem = {}
        self.dcnt = {}
        self.waited = {e: {} for e in self.eng}
        self.lw = {}
        self.rd = {}

    def _wait(self, e, toks):
        need = {}
        for tk in toks:
            if tk is None:
                continue
            sk, val = tk
            if sk == e and e == 'pe':
                continue
            if need.get(sk, 0) < val:
                need[sk] = val
        for sk, val in need.items():
            if self.waited[e].get(sk, 0) >= val:
                continue
            self.waited[e][sk] = val
            s = self.sem[sk] if sk in self.sem else self.dsem[sk]
            self.eng[e].wait_ge(s, val)

    def _deps(self, reads, writes):
        toks = []
        for k in reads:
            toks.append(self.lw.get(k))
        for k in writes:
            toks.append(self.lw.get(k))
            toks.extend(self.rd.get(k, ()))
        return toks

    def _upd(self, tok, reads, writes):
        for k in reads:
            self.rd.setdefault(k, []).append(tok)
        for k in writes:
            self.lw[k] = tok
            self.rd[k] = []

    def op(self, e, fn, reads=(), writes=()):
        self._wait(e, self._deps(reads, writes))
        ins = fn(self.eng[e])
        self.cnt[e] += 1
        ins.then_inc(self.sem[e], 1)
        self._upd((e, self.cnt[e]), reads, writes)

    def dma(self, q, cls, out, in_, reads=(), writes=()):
        if cls not in self.dsem:
            self.dsem[cls] = self.es.enter_context(self.nc.semaphore('d_' + cls))
            self.dcnt[cls] = 0
        self._wait(q, self._deps(reads, writes))
        ins = self.eng[q].dma_start(out=out, in_=in_)
        self.dcnt[cls] += 16
        ins.then_inc(self.dsem[cls], 16)
        self._upd((cls, self.dcnt[cls]), reads, writes)

    def barrier(self):
        toks = [(c, v) for c, v in self.dcnt.items() if not (len(c) == 2 and c[0] == 'w' and c[1].isdigit())] + [(e, v) for e, v in self.cnt.items() if v > 0]
        for e in self.eng:
            self._wait(e, [tk for tk in toks if tk[0] != e])

    def collective(self, cls, fn, reads=(), writes=()):
        if cls not in self.dsem:
            self.dsem[cls] = self.es.enter_context(self.nc.semaphore('d_' + cls))
            self.dcnt[cls] = 0
        self._wait('pool', self._deps(reads, writes))
        ins = fn(self.eng['pool'])
        self.dcnt[cls] += 1
        ins.then_inc(self.dsem[cls], 1)
        self._upd((cls, self.dcnt[cls]), reads, writes)

    def finish(self):
        toks = [(c, v) for c, v in self.dcnt.items()] + [(e, v) for e, v in self.cnt.items() if e != 'sp' and v > 0]
        self._wait('sp', toks)


def host_consts(npt, pos0, spos0, hseq=0):
    nt = npt + 1
    half = 128
    inv = (np.float32(10000.0) ** (-(np.arange(half, dtype=np.float32) / np.float32(half)))).astype(np.float32)
    pos = np.zeros((128, nt), np.float32)
    il = np.zeros((128, nt), np.float64)
    for t in range(npt):
        pos[:, t] = pos0 + t * 128 + np.arange(128)
        il[:, t] = np.arange(128)
    pos[:, npt] = spos0 + (np.arange(128) % 4)
    il[:, npt] = np.arange(128) % 4
    ang = (pos[:, :, None] * inv[None, None, :]).astype(np.float32)
    cos = np.cos(ang).astype(np.float32); sin = np.sin(ang).astype(np.float32)
    dqk = np.zeros((128, nt, 8), np.float32)
    for h in range(4):
        g = np.float64(np.float32(np.log1p(np.float32(-2.0 ** (-5 - h)))))
        dqk[:, :, h] = np.exp(g * (il + 1.0))
        dqk[:, :, 4 + h] = np.exp(-g * (il + 1.0)) / 16.0
    j = np.arange(128)
    maskT = (j[None, :] >= j[:, None]).astype(np.float32)
    js = np.arange(64)
    maskS = ((js[None, :] >= js[:, None]) & (js[None, :] // 4 == js[:, None] // 4)).astype(np.float32)
    mb = (js[:, None] // 4 == np.arange(16)[None, :]).astype(np.float32)
    pA = np.zeros((128, 4, 4, 128), np.float32)
    pAs = np.zeros((64, 4, 64), np.float32)
    pBs = np.zeros((120, 2, 4, 64), np.float32)
    for g, w in enumerate(WINS):
        for t in range(128):
            for s in range(max(0, t - w + 1), t + 1):
                pA[s, g, 0, t] += 1.0 / min(t + 1, w)
                pA[s, g, 1, t] += 1.0 / w
            pA[t, g, 0, t] -= 1.0
            pA[t, g, 1, t] -= 1.0
            for sl in range(128):
                if sl - 128 >= t - w + 1:
                    pA[sl, g, 2, t] = 1.0 / w
        if hseq == 1:
            pA[:, g, 0, :] = pA[:, g, 1, :]
            pA[:, g, 3, :] = pA[:, g, 2, :]
        for b in range(16):
            for r in range(4):
                t = 4 * b + r
                for r2 in range(0, r + 1):
                    if r - r2 < w:
                        pAs[4 * b + r2, g, t] += 1.0 / w
                pAs[t, g, t] -= 1.0
                for k in range(15):
                    if k >= 16 + r - w:
                        pBs[(b % 8) * 15 + k, b // 8, g, t] = 1.0 / w
    bf = ml_dtypes.bfloat16
    csc = np.zeros((128, npt + 1, 4), np.float32)
    for h in range(4):
        g_ = np.float64(np.float32(np.log1p(np.float32(-2.0 ** (-5 - h)))))
        for t in range(npt + 1):
            csc[:, t, h] = hseq * np.exp(g_ * 128.0 * t)
    return {"c_csc": csc, "c_cos": cos, "c_sin": sin, "c_dqk": dqk, "c_identb": np.eye(128, dtype=np.float32).astype(bf),
            "c_maskT": maskT, "c_maskS": maskS, "c_mb": mb, "c_pA": pA.astype(bf), "c_pAs": pAs.astype(bf),
            "c_pBs": pBs.astype(bf)}


def build(stop_after=None, dbg=False, ncores=8):
    nc = bass.Bass("TRN2", target_bir_lowering=False)

    def din(name, shape, dt=F32):
        return nc.dram_tensor(name, list(shape), dt, kind="ExternalInput").ap()

    def dout(name, shape):
        return nc.dram_tensor(name, list(shape), F32, kind="ExternalOutput").ap()

    def dscr(name, shape, dt=F32):
        return nc.dram_tensor(name, list(shape), dt).ap()

    xp = din("xp", [NPT * 128, D]); xs = din("xs", [64, D]); mem = din("mem", [256, D])
    sret = din("sret", [2, 16, 4, 256, 256]); spool = din("spool", [2, 16, 15, 1024])
    ck = din("ck", [2, 16, 256, D]); cv = din("cv", [2, 16, 256, D])
    w_in = din("w_in", [2, D, 5120]); pool_w = din("pool_w", [2, 4, 256, 256])
    w_out = din("w_out", [2, D, D]); w_xq = din("w_xq", [2, D, D]); w_mk = din("w_mk", [2, D, D])
    w_mv = din("w_mv", [2, D, D]); w_xo = din("w_xo", [2, D, D])
    w_up = din("w_up", [2, D, 8192]); w_down = din("w_down", [2, 8192, D])
    normw = din("normw", [9, D])
    retw = din("retw", [2, 1024]); pscale = din("pscale", [2, 1024])
    c_cos = din("c_cos", [128, NT, 128]); c_sin = din("c_sin", [128, NT, 128])
    c_dqk = din("c_dqk", [128, NT, 8])
    c_identb = din("c_identb", [128, 128], BF16)
    c_maskT = din("c_maskT", [128, 128]); c_maskS = din("c_maskS", [64, 64])
    c_mb = din("c_mb", [64, 16])
    c_pA = din("c_pA", [128, 4, 4, 128], BF16)
    c_csc = din("c_csc", [128, NT, 4])
    c_pAs = din("c_pAs", [64, 4, 64], BF16)
    c_pBs = din("c_pBs", [120, 2, 4, 64], BF16)

    y_p = dout("y_p", [NPT * 128, D]); y_s = dout("y_s", [64, D])
    o_retp = dout("o_retp", [2, 4, 256, 256]); o_bufp = dout("o_bufp", [2, 15, 1024])
    o_mk = dout("o_mk", [2, 256, D]); o_mv = dout("o_mv", [2, 256, D])
    o_rets = dout("o_rets", [2, 16, 4, 256, 256]); o_bufs = dout("o_bufs", [2, 16, 15, 1024])
    dbg_cat = nc.dram_tensor("dbg_cat", [TOK, D], BF16, kind="ExternalOutput").ap() if dbg else None
    dbg_x = dout("dbg_x", [TOK, D]) if dbg else None

    X = dscr("X", [TOK, D])
    QK = dscr("QK", [TOK, 2048], BF16); Vd = dscr("Vd", [TOK, 1024], BF16); Gd = dscr("Gd", [TOK, 1024], BF16)
    Ud = dscr("Ud", [TOK, 1024]); XQ = dscr("XQ", [TOK, D], BF16)
    ATd = dscr("ATd", [8192, TOK], BF16)
    OL = dscr("OL", [NPT * 128, 1024])
    ib_t = nc.dram_tensor("xch_in", [1088, 256], F32)
    ob_t = nc.dram_tensor("xch_out", [2176, 256], F32)
    ib = ib_t.ap(); ob = ob_t.ap()

    TG = [(c0, min(512, TOK - c0)) for c0 in range(0, TOK, 512)]

    es = contextlib.ExitStack()
    with es:
        tr = Tr(nc, es)
        uid = [0]

        def sbx(stack, shape, dt=F32):
            uid[0] += 1
            return stack.enter_context(nc.sbuf_tensor(f"t{uid[0]}", list(shape), dt))

        sb = lambda shape, dt=F32: sbx(es, shape, dt)

        class Stage:
            def __enter__(self):
                self.st = contextlib.ExitStack()
                self.st.__enter__()
                return lambda shape, dt=F32: sbx(self.st, shape, dt)

            def __exit__(self, *a):
                tr.barrier()
                self.st.__exit__(*a)
                return False

        XT = sb([128, 16, TOK], BF16)
        NW = 2
        wbuf = sb([128, NW, 16, 512], BF16)
        dqk = sb([128, NT, 8])
        identb = sb([128, 128], BF16)
        maskT = sb([128, 128]); maskS = sb([64, 64]); mbm = sb([64, 16])
        pA = sb([128, 4, 4, 128], BF16); csc = sb([128, NT, 4]); pAs = sb([64, 4, 64], BF16); pBs = sb([120, 2, 4, 64], BF16)
        wrow2 = sb([128, 1024]); wrow3 = sb([128, 1024])
        poolw = sb([128, 4, 2, 256], BF16)
        zero_b = sb([128, 512], BF16)
        ev = [sb([128, 512]) for _ in range(3)]
        evb = [sb([128, 512], BF16) for _ in range(3)]
        st = [sb([128, 8]) for _ in range(2)]
        PSALL = es.enter_context(nc.psum_tensor("psall", [128, 8, 512], F32))
        PB = [PSALL[:, i, :] for i in range(8)]

        def bfv(bank):
            return bank[:, :].bitcast(BF16)

        for (dst, src) in [(csc, c_csc), (dqk, c_dqk), (identb, c_identb), (maskT, c_maskT),
                           (maskS, c_maskS), (mbm, c_mb), (pA, c_pA), (pAs, c_pAs), (pBs, c_pBs)]:
            tr.dma('sp', 'c_' + dst.name, dst[:], src, writes=[('c', dst.name)])
        tr.op('dve', lambda e: e.memset(zero_b[:], 0.0), writes=['zero_b'])
        CK = [('c', t_.name) for t_ in (csc, dqk, identb, maskT, maskS, mbm, pA, pAs, pBs)] + ['zero_b']

        xkeys = lambda t: [('X', t, nb) for nb in range(4)]
        tr.dma('sp', 'xinit0', X[0:NPT * 128, :], xp, writes=[k for t in range(NPT) for k in xkeys(t)])
        tr.dma('sp', 'xinit1', X[SROW:TOK, :], xs, writes=xkeys(NPT))

        wstate = {'n': 0}

        def wload(src_ap, nk):
            s = wstate['n'] % NW
            wstate['n'] += 1
            tr.dma('pool', 'w%d' % s, wbuf[:, s, 0:nk, :], src_ap.rearrange("(c p) n -> p c n", p=128), writes=[('w', s)])
            return s

        cnt = {}

        def nxt(k, n):
            v = cnt.get(k, 0) % n
            cnt[k] = cnt.get(k, 0) + 1
            return v

        def transpose_to(dst_fn, src_tile, rows, nchunks, rk, wk):
            for c0 in range(0, nchunks, 8):
                n = min(8, nchunks - c0)
                b = 6 + nxt('x', 2)
                pv = bfv(PB[b])

                def f(e, c0=c0, n=n, pv=pv):
                    for j in range(n):
                        ins = e.transpose(pv[:, j * 128:j * 128 + rows], src_tile[0:rows, (c0 + j) * 128:(c0 + j + 1) * 128],
                                          identb[0:rows, 0:rows])
                    return ins
                tr.op('pe', f, reads=list(rk) + CK, writes=[('pb', b)])
                src = pv[:, 0:n * 128].rearrange("p (j t) -> p j t", t=128)[:, :, 0:rows]
                if nxt('xe', 2) == 0:
                    tr.op('act', lambda e, src=src, c0=c0, n=n: e.copy(dst_fn(c0, n), src), reads=[('pb', b)], writes=list(wk))
                else:
                    tr.op('dve', lambda e, src=src, c0=c0, n=n: e.tensor_copy(dst_fn(c0, n), src), reads=[('pb', b)], writes=list(wk))

        def rms_rstd(src, rows, width, sidx, rk, junk):
            s = st[sidx]
            tr.op('act', lambda e: e.activation(junk[0:rows, 0:width], src, AF.Square, accum_out=s[0:rows, 0:1]),
                  reads=list(rk), writes=['junk', ('st', sidx)])
            tr.op('dve', lambda e: e.tensor_scalar(s[0:rows, 1:2], s[0:rows, 0:1], 1.0 / width, EPS, ALU.mult, ALU.add),
                  reads=[('st', sidx)], writes=[('st', sidx)])
            tr.op('act', lambda e: e.activation(s[0:rows, 2:3], s[0:rows, 1:2], AF.Sqrt), reads=[('st', sidx)], writes=[('st', sidx)])
            tr.op('dve', lambda e: e.reciprocal(s[0:rows, 3:4], s[0:rows, 2:3]), reads=[('st', sidx)], writes=[('st', sidx)])
            return s[0:rows, 3:4]

        def load_wrow(dst, src_row, key):
            tr.dma('sp', 'wr_' + key, dst[:], src_row.partition_broadcast(128), writes=[key])

        def norm_stage(src_dram, ntiles, rows_fn, wsrc, dst_fn, wk_fn, xkeyfn, out_dram=None):
            with Stage() as S:
                wrow = S([128, D])
                load_wrow(wrow, wsrc, 'wrow')
                xt_in = [S([128, D]) for _ in range(2)]
                hb = [S([128, D], BF16) for _ in range(2)]
                junk = S([128, D], BF16)
                for t in range(ntiles):
                    rows = rows_fn(t)
                    i = t % 2
                    tr.dma('sp', 'xtin%d' % i, xt_in[i][0:rows, :], src_dram[t * 128:t * 128 + rows, :], reads=xkeyfn(t), writes=[('xt_in', i)])
                    r = rms_rstd(xt_in[i][0:rows, :], rows, D, i, [('xt_in', i)], junk)
                    if out_dram is None:
                        tr.op('dve', lambda e, i=i, rows=rows, r=r: e.scalar_tensor_tensor(
                            hb[i][0:rows, :], xt_in[i][0:rows, :], r, wrow[0:rows, :], ALU.mult, ALU.mult),
                            reads=[('xt_in', i), ('st', i), 'wrow'], writes=[('hb', i)])
                        transpose_to(lambda c0, n, t=t, rows=rows: dst_fn(t, rows, c0, n), hb[i], rows, 16, [('hb', i)], wk_fn(t))
                    else:
                        tr.op('dve', lambda e, i=i, rows=rows, r=r: e.scalar_tensor_tensor(
                            xt_in[i][0:rows, :], xt_in[i][0:rows, :], r, wrow[0:rows, :], ALU.mult, ALU.mult),
                            reads=[('xt_in', i), ('st', i), 'wrow'], writes=[('xt_in', i)])
                        tr.dma('sp', 'xtin%d' % i, out_dram(t, rows), xt_in[i][0:rows, :], reads=[('xt_in', i)], writes=[('y', t)])

        xt_dst = lambda t, rows, c0, n: XT[:, c0:c0 + n, t * 128:t * 128 + rows]
        xt_wk = lambda t: [('XT', t)]

        def linear_tok(ntiles, rows_fn, W, ncols, epilogue, src_fn=None, nk=16, key='XT', prefetch=None):
            sf = src_fn or (lambda t, rows, kc: XT[:, kc, t * 128:t * 128 + rows])
            for nb in range(ncols // 512):
                s = wload(W[:, nb * 512:(nb + 1) * 512], nk)
                if prefetch:
                    prefetch(0, rows_fn(0), nb)
                for t in range(ntiles):
                    rows = rows_fn(t)
                    b = nxt('pb', 4)

                    def f(e, t=t, rows=rows, b=b, s=s):
                        for kc in range(nk):
                            ins = e.matmul(PB[b][0:rows, :], sf(t, rows, kc), wbuf[:, s, kc, :], start=(kc == 0), stop=(kc == nk - 1))
                        return ins
                    tr.op('pe', f, reads=[(key, t), ('w', s)], writes=[('pb', b)])
                    if prefetch and t + 1 < ntiles:
                        prefetch(t + 1, rows_fn(t + 1), nb)
                    epilogue(t, rows, nb, b)

        rbuf = {}

        def resid_prefetch(t, rows, nb):
            i = nxt('ev', 3)
            rbuf[(t, nb)] = i
            tr.dma('sp', 'ev%d' % i, ev[i][0:rows, :], X[t * 128:t * 128 + rows, nb * 512:(nb + 1) * 512],
                   reads=[('X', t, nb)], writes=[('ev', i)])

        def resid_epilogue(t, rows, nb, b):
            i = rbuf.pop((t, nb))
            tr.op('dve', lambda e: e.tensor_tensor(ev[i][0:rows, :], ev[i][0:rows, :], PB[b][0:rows, :], ALU.add),
                  reads=[('pb', b), ('ev', i)], writes=[('ev', i)])
            tr.dma('sp', 'ev%d' % i, X[t * 128:t * 128 + rows, nb * 512:(nb + 1) * 512], ev[i][0:rows, :],
                   reads=[('ev', i)], writes=[('X', t, nb)])

        def linear_resid(W, nk=16):
            linear_tok(NT, trows, W, D, resid_epilogue, nk=nk, prefetch=resid_prefetch)

        for l in range(2):
            tr.dma('pool', 'poolw', poolw[:], pool_w[l].rearrange("g (cc p) d -> p g cc d", p=128), writes=['poolw'])
            load_wrow(wrow2, retw[l], 'wrow2')
            load_wrow(wrow3, pscale[l], 'wrow3')

            norm_stage(X, NT, trows, normw[l * 4 + 0], xt_dst, xt_wk, xkeys)

            with Stage() as S:
                tas = [S([128, 512]) for _ in range(2)]; tbs = [S([128, 512]) for _ in range(2)]
                nsin_t = S([128, NT, 128])
                cos_t = S([128, NT, 128]); sin_t = S([128, NT, 128])
                tr.dma('sp', 'cos', cos_t[:], c_cos, writes=['cos'])
                tr.dma('sp', 'sin', sin_t[:], c_sin, writes=['sin'])
                tr.op('dve', lambda e: e.tensor_scalar(nsin_t[:], sin_t[:], -1.0, 0.0, ALU.mult, ALU.add), reads=['sin'], writes=['nsin'])

                def inproj_ep(t, rows, nb, b):
                    P = PB[b]
                    i = nxt('ev', 3)
                    if nb < 4:
                        isk = nb >= 2
                        k2 = nxt('rope', 2)
                        ta, tb = tas[k2], tbs[k2]
                        P3 = P[0:rows, :].rearrange("p (c f) -> p c f", f=128)
                        P4 = P[0:rows, :].rearrange("p (hh hf f) -> p hh hf f", hh=2, hf=2)
                        cos4 = cos_t[0:rows, t:t + 1, :].broadcast_to([rows, 4, 128])
                        sin2 = sin_t[0:rows, t:t + 1, :].broadcast_to([rows, 2, 128])
                        nsin2 = nsin_t[0:rows, t:t + 1, :].broadcast_to([rows, 2, 128])
                        ta3 = ta[0:rows, :].rearrange("p (c f) -> p c f", f=128)
                        tb4 = tb[0:rows, :].rearrange("p (hh hf f) -> p hh hf f", hh=2, hf=2)
                        tr.op('dve', lambda e: e.tensor_tensor(ta3, P3, cos4, ALU.mult), reads=[('pb', b), 'cos'] + CK, writes=[('ta', k2)])
                        tr.op('dve', lambda e: e.tensor_tensor(tb4[:, :, 0, :], P4[:, :, 1, :], nsin2, ALU.mult), reads=[('pb', b), 'nsin'], writes=[('tb', k2, 0)])
                        tr.op('dve', lambda e: e.tensor_tensor(tb4[:, :, 1, :], P4[:, :, 0, :], sin2, ALU.mult), reads=[('pb', b), 'sin'], writes=[('tb', k2, 1)])
                        tr.op('dve', lambda e: e.tensor_tensor(ta[0:rows, :], ta[0:rows, :], tb[0:rows, :], ALU.add),
                              reads=[('ta', k2), ('tb', k2, 0), ('tb', k2, 1)], writes=[('ta', k2)])
                        for hh in range(2):
                            h = (nb % 2) * 2 + hh
                            ci = (4 if isk else 0) + h
                            dcol = dqk[0:rows, t, ci:ci + 1]
                            tr.op('act', lambda e, hh=hh, dcol=dcol: e.activation(evb[i][0:rows, hh * 256:(hh + 1) * 256], ta[0:rows, hh * 256:(hh + 1) * 256],
                                                                              AF.Copy, scale=dcol),
                                  reads=[('ta', k2)] + CK, writes=[('evb', i)])
                        tr.dma('sp', 'evb%d' % i, QK[t * 128:t * 128 + rows, nb * 512:(nb + 1) * 512], evb[i][0:rows, :],
                               reads=[('evb', i)], writes=[('QK', t, nb)])
                    elif nb < 6:
                        tr.op('act', lambda e: e.copy(evb[i][0:rows, :], P[0:rows, :]), reads=[('pb', b)], writes=[('evb', i)])
                        tr.dma('sp', 'evb%d' % i, Vd[t * 128:t * 128 + rows, (nb - 4) * 512:(nb - 3) * 512], evb[i][0:rows, :],
                               reads=[('evb', i)], writes=[('V', t, nb)])
                    elif nb < 8:
                        tr.op('act', lambda e: e.activation(evb[i][0:rows, :], P[0:rows, :], AF.Silu), reads=[('pb', b)], writes=[('evb', i)])
                        tr.dma('sp', 'evb%d' % i, Gd[t * 128:t * 128 + rows, (nb - 6) * 512:(nb - 5) * 512], evb[i][0:rows, :],
                               reads=[('evb', i)], writes=[('G', t, nb)])
                    else:
                        tr.op('act', lambda e: e.copy(ev[i][0:rows, :], P[0:rows, :]), reads=[('pb', b)], writes=[('ev', i)])
                        tr.dma('sp', 'ev%d' % i, Ud[t * 128:t * 128 + rows, (nb - 8) * 512:(nb - 7) * 512], ev[i][0:rows, :],
                               reads=[('ev', i)], writes=[('U', t, nb)])

                linear_tok(NT, trows, w_in[l], 5120, inproj_ep)
            qkk = lambda t: [('QK', t, nb) for nb in range(4)]
            vk = lambda t: [('V', t, 4), ('V', t, 5)]
            gk = lambda t: [('G', t, 6), ('G', t, 7)]
            uk = lambda t: [('U', t, 8), ('U', t, 9)]

            tr.dma('sp', 'ob_a%d' % l, o_bufp[l], Ud[NPT * 128 - 15:NPT * 128, :], reads=uk(NPT - 1), writes=[('o_bufp', l)])
            tr.dma('sp', 'ob_b%d' % l, o_bufs[l][:, 0:11, :], spool[l][:, 4:15, :], writes=[('o_bufs', l, 0)])
            tr.dma('sp', 'ob_c%d' % l, o_bufs[l][:, 11:15, :], Ud[SROW:TOK, :].rearrange("(b r) n -> b r n", r=4),
                   reads=uk(NPT), writes=[('o_bufs', l, 1)])
            if stop_after == 'S2':
                break

            with Stage() as S:
                qk_sb = [S([128, 2048], BF16) for _ in range(2)]
                v_sb = [S([128, 1024], BF16) for _ in range(2)]
                g_sb = [S([128, 1024], BF16) for _ in range(2)]
                ub = [S([128, 1024], BF16) for _ in range(3)]
                ubp = S([128, 1024], BF16)
                olt = [S([128, 1024]) for _ in range(2)]
                cat = [S([128, D], BF16) for _ in range(2)]
                qT = S([128, 2, 128], BF16); kT = S([128, 2, 128], BF16); sT = S([128, 128], BF16)
                vg = S([128, 256], BF16)
                S32 = S([128, 4, 2, 256]); Sbf = S([128, 4, 2, 256], BF16)
                Sh32 = S([128, 4, 2, 256]); Shb = S([128, 4, 2, 256], BF16)
                S0 = [S([128, 2, 256]) for _ in range(4)]; S0b = [S([128, 2, 256], BF16) for _ in range(4)]
                vmk = [S([64, 256], BF16) for _ in range(2)]
                qTzw = S([128, 2, 16, 124], BF16)
                on = S([128, 256]); osb = S([128, 256]); junk = S([128, 256], BF16)
                qT4 = S([128, 8, 128], BF16); kT4 = S([128, 8, 128], BF16); sT4 = S([128, 4, 128], BF16)
                st2 = S([128, 8])
                vg4 = S([128, 4, 256], BF16); osb4 = S([128, 4, 256]); on4 = S([128, 4 * 256])
                OC4 = PSALL[:, 4:6, :].rearrange("p b (h m) -> p (b h) m", m=256)
                bufb = S([120, 2, 1024], BF16)
                pT = S([128, 8, 128], BF16)

                tr.op('dve', lambda e: e.memset(S32[:], 0.0), writes=['S32'])
                tr.op('dve', lambda e: e.memset(Sbf[:], 0.0), writes=['Sb'])
                tr.op('dve', lambda e: e.memset(qTzw[:], 0.0), writes=['qTzw'])
                tr.op('dve', lambda e: e.memset(ubp[:], 0.0), writes=['ubp'])
                tr.dma('pool', 'bufb', bufb[:], spool[l].rearrange("(k b) r n -> (b r) k n", k=2), writes=['bufb'])

                def s3_load(t):
                    rows = trows(t); i = t % 2
                    tr.dma('sp', 'qk%d' % i, qk_sb[i][0:rows, :], QK[t * 128:t * 128 + rows, :], reads=qkk(t), writes=[('qk_sb', i)])
                    tr.dma('sp', 'v%d' % i, v_sb[i][0:rows, :], Vd[t * 128:t * 128 + rows, :], reads=vk(t), writes=[('v_sb', i)])
                    if t == NPT:
                        tr.dma('sp', 'g%d' % i, g_sb[i][0:rows, :], Gd[t * 128:t * 128 + rows, :], reads=gk(t), writes=[('g_sb', i)])
                        tr.dma('pool', 'ub%d' % (t % 3), ub[t % 3][0:rows, :], Ud[t * 128:t * 128 + rows, :], reads=uk(t), writes=[('ub', t % 3)])

                def head_finish(rows, h, osrc, okey, i):
                    r = rms_rstd(osrc, rows, 256, 0, [okey], junk)
                    tr.op('dve', lambda e: e.scalar_tensor_tensor(on[0:rows, :], osrc, r, wrow2[0:rows, h * 256:h * 256 + 256], ALU.mult, ALU.mult),
                          reads=[okey, ('st', 0), 'wrow2'], writes=['on'])
                    tr.op('dve', lambda e: e.tensor_tensor(cat[i][0:rows, h * 256:h * 256 + 256], on[0:rows, :],
                                                          g_sb[i][0:rows, h * 256:h * 256 + 256], ALU.mult),
                          reads=['on', ('g_sb', i)], writes=[('cat', i)])

                def qk_transposes(q_, rows, h, i, with_k=True):
                    def f(e, h=h):
                        pv = bfv(PB[3])
                        for dc in range(2):
                            ins = e.transpose(pv[:, dc * 128:dc * 128 + rows], q_[0:rows, h * 256 + dc * 128:h * 256 + dc * 128 + 128],
                                              identb[0:rows, 0:rows])
                        if with_k:
                            for dc in range(2):
                                ins = e.transpose(pv[:, 256 + dc * 128:256 + dc * 128 + rows],
                                                  q_[0:rows, 1024 + h * 256 + dc * 128:1024 + h * 256 + dc * 128 + 128],
                                                  identb[0:rows, 0:rows])
                        return ins
                    tr.op('pe', f, reads=[('qk_sb', i)] + CK, writes=[('pb', 3)])
                    pv3 = bfv(PB[3])
                    tr.op('act', lambda e: e.copy(qT[:, :, 0:rows], pv3[:, 0:256].rearrange("p (a b) -> p a b", b=128)[:, :, 0:rows]),
                          reads=[('pb', 3)], writes=['qT'])
                    if with_k:
                        tr.op('act', lambda e: e.copy(kT[:, :, 0:rows], pv3[:, 256:512].rearrange("p (a b) -> p a b", b=128)[:, :, 0:rows]),
                              reads=[('pb', 3)], writes=['kT'])

                def pooling(t, rows, i, smp):
                    u3 = t % 3; p3 = (t - 1) % 3
                    for g in range(4):
                        def f(e, g=g):
                            for cc in range(2):
                                ch = 2 * g + cc
                                out = PB[1][:, cc * 128:cc * 128 + rows]
                                if smp:
                                    e.matmul(out, ub[u3][0:64, ch * 128:ch * 128 + 128], pAs[:, g, :], start=True, stop=False)
                                    for kk in range(2):
                                        ins = e.matmul(out, bufb[:, kk, ch * 128:ch * 128 + 128], pBs[:, kk, g, :], start=False, stop=(kk == 1))
                                elif t == 0:
                                    e.matmul(out, ub[u3][:, ch * 128:ch * 128 + 128], pA[:, g, 0, :], start=True, stop=False)
                                    ins = e.matmul(out, ubp[:, ch * 128:ch * 128 + 128], pA[:, g, 3, :], start=False, stop=True)
                                else:
                                    e.matmul(out, ub[u3][:, ch * 128:ch * 128 + 128], pA[:, g, 1, :], start=True, stop=False)
                                    ins = e.matmul(out, ub[p3][:, ch * 128:ch * 128 + 128], pA[:, g, 2, :], start=False, stop=True)
                            return ins
                        tr.op('pe', f, reads=[('ub', u3), ('ub', p3), 'bufb', 'ubp'] + CK, writes=[('pb', 1)])
                        tr.op('act', lambda e, g=g: e.copy(pT[:, 2 * g:2 * g + 2, 0:rows],
                                                           PB[1][:, 0:256].rearrange("p (a b) -> p a b", b=128)[:, :, 0:rows]),
                              reads=[('pb', 1)], writes=[('pT', g)])

                        def f(e, g=g):
                            for cc in range(2):
                                ins = e.matmul(PB[0][0:rows, 0:256], pT[:, 2 * g + cc, 0:rows], poolw[:, g, cc, :], start=(cc == 0), stop=(cc == 1),
                                               skip_group_check=True)
                            return ins
                        tr.op('pe', f, reads=[('pT', g), 'poolw'], writes=[('pb', 0)])
                        tr.op('dve', lambda e, g=g: e.tensor_tensor(cat[i][0:rows, 1024 + g * 256:1024 + g * 256 + 256], PB[0][0:rows, 0:256],
                                                                  wrow3[0:rows, g * 256:g * 256 + 256], ALU.mult),
                              reads=[('pb', 0), 'wrow3'], writes=[('cat', i)])

                def passA_prompt(t, i):
                    q_ = qk_sb[i]; v_ = v_sb[i]
                    def f(e):
                        pq = bfv(PB[3]); pk = bfv(PB[7])
                        for c in range(8):
                            e.transpose(pq[:, c * 128:(c + 1) * 128], q_[:, c * 128:(c + 1) * 128], identb[:, :])
                        for c in range(8):
                            ins = e.transpose(pk[:, c * 128:(c + 1) * 128], q_[:, 1024 + c * 128:1024 + (c + 1) * 128], identb[:, :])
                        return ins
                    tr.op('pe', f, reads=[('qk_sb', i)] + CK, writes=[('pb', 3), ('pb', 7)])
                    tr.op('act', lambda e: e.copy(qT4[:, :, :], bfv(PB[3]).rearrange("p (a b) -> p a b", b=128)), reads=[('pb', 3)], writes=['qT4'])
                    tr.op('act', lambda e: e.copy(kT4[:, :, :], bfv(PB[7]).rearrange("p (a b) -> p a b", b=128)), reads=[('pb', 7)], writes=['kT4'])

                    def f(e):
                        for h in range(4):
                            for dc in range(2):
                                ins = e.matmul(PB[2][:, h * 128:(h + 1) * 128], kT4[:, 2 * h + dc, :], qT4[:, 2 * h + dc, :],
                                               start=(dc == 0), stop=(dc == 1), skip_group_check=True)
                        return ins
                    tr.op('pe', f, reads=['qT4', 'kT4'], writes=[('pb', 2)])
                    for h in range(4):
                        tr.op('dve', lambda e, h=h: e.tensor_tensor(sT4[:, h, :], PB[2][:, h * 128:(h + 1) * 128], maskT[:, :], ALU.mult),
                              reads=[('pb', 2)] + CK, writes=[('sT4', h)])

                    def f(e):
                        for h in range(4):
                            O = PB[4 + h // 2][:, (h % 2) * 256:(h % 2) * 256 + 256]
                            e.matmul(O, sT4[:, h, :], v_[:, h * 256:h * 256 + 256], start=True, stop=False, skip_group_check=True)
                            for dc in range(2):
                                ins = e.matmul(O, qT4[:, 2 * h + dc, :], Sbf[:, h, dc, :], start=False, stop=(dc == 1), skip_group_check=True)
                        return ins
                    tr.op('pe', f, reads=[('sT4', h) for h in range(4)] + [('v_sb', i), 'qT4', 'Sb'], writes=[('pb', 4), ('pb', 5)])
                    tr.op('act', lambda e: e.copy(olt[i][:, :], PSALL[:, 4:6, :].rearrange("p b n -> p (b n)")),
                          reads=[('pb', 4), ('pb', 5)], writes=[('olt', i)])
                    for h in range(4):
                        tr.op('act', lambda e, h=h: e.activation(vg4[:, h, :], v_[:, h * 256:h * 256 + 256], AF.Copy, scale=GAM[h] ** 128),
                              reads=[('v_sb', i)], writes=[('vg4', h)])
                    for h in range(4):
                        ub_ = h % 2

                        def f(e, h=h, ub_=ub_):
                            for dc in range(2):
                                ins = e.matmul(PB[ub_][:, 256 * dc:256 * dc + 256],
                                               q_[:, 1024 + h * 256 + dc * 128:1024 + h * 256 + dc * 128 + 128], vg4[:, h, :],
                                               start=True, stop=True, skip_group_check=True)
                            return ins
                        tr.op('pe', f, reads=[('qk_sb', i), ('vg4', h)], writes=[('pb', ub_)])
                        tr.op('dve', lambda e, h=h, ub_=ub_: e.scalar_tensor_tensor(
                            S32[:, h, :, :], S32[:, h, :, :], GAM[h] ** 128, PB[ub_][:, :].rearrange("p (a b) -> p a b", b=256), ALU.mult, ALU.add),
                            reads=[('pb', ub_), 'S32'], writes=['S32'])
                    tr.op('act', lambda e: e.copy(Sbf[:], S32[:]), reads=['S32'], writes=['Sb'])
                    tr.dma('sp', 'ol%d' % i, OL[t * 128:(t + 1) * 128, :], olt[i][:, :], reads=[('olt', i)], writes=[('OL', t)])

                s3_load(0)
                for t in range(NT):
                    rows = trows(t); i = t % 2
                    if t + 1 < NT:
                        s3_load(t + 1)
                    q_ = qk_sb[i]; v_ = v_sb[i]
                    smp = (t == NPT)
                    if not smp:
                        passA_prompt(t, i)
                        continue
                    for h in range(4):
                        qk_transposes(q_, rows, h, i)

                        def f(e):
                            for dc in range(2):
                                ins = e.matmul(PB[2][0:rows, 0:rows], kT[:, dc, 0:rows], qT[:, dc, 0:rows], start=(dc == 0), stop=(dc == 1))
                            return ins
                        tr.op('pe', f, reads=['qT', 'kT'], writes=[('pb', 2)])
                        msk = maskS if smp else maskT
                        tr.op('dve', lambda e, msk=msk: e.tensor_tensor(sT[0:rows, 0:rows], PB[2][0:rows, 0:rows], msk[0:rows, 0:rows], ALU.mult),
                              reads=[('pb', 2)] + CK, writes=['sT'])
                        ob_ = 4 + (h % 2)
                        O = PB[ob_][0:rows, 0:256]
                        if not smp:
                            def f(e, h=h, O=O):
                                e.matmul(O, sT[0:rows, 0:rows], v_[0:rows, h * 256:h * 256 + 256], start=True, stop=False)
                                for dc in range(2):
                                    ins = e.matmul(O, qT[:, dc, 0:rows], Sbf[:, h, dc, :], start=False, stop=(dc == 1))
                                return ins
                            tr.op('pe', f, reads=['sT', ('v_sb', i), 'qT', 'Sb'], writes=[('pb', ob_)])
                            tr.op('act', lambda e, h=h, O=O: e.copy(olt[i][:, h * 256:h * 256 + 256], O), reads=[('pb', ob_)], writes=[('olt', i)])
                            g128 = GAM[h] ** 128
                            tr.op('act', lambda e, h=h, g128=g128: e.activation(vg[0:rows, :], v_[0:rows, h * 256:h * 256 + 256], AF.Copy, scale=g128),
                                  reads=[('v_sb', i)], writes=['vg'])

                            def f(e, h=h):
                                for dc in range(2):
                                    ins = e.matmul(PB[0][:, 256 * dc:256 * dc + 256],
                                                   q_[0:rows, 1024 + h * 256 + dc * 128:1024 + h * 256 + dc * 128 + 128], vg[0:rows, :],
                                                   start=True, stop=True, skip_group_check=True)
                                return ins
                            tr.op('pe', f, reads=[('qk_sb', i), 'vg'], writes=[('pb', 0)])
                            tr.op('dve', lambda e, h=h, g128=g128: e.scalar_tensor_tensor(
                                S32[:, h, :, :], S32[:, h, :, :], g128, PB[0][:, :].rearrange("p (a b) -> p a b", b=256), ALU.mult, ALU.add),
                                reads=[('pb', 0), 'S32'], writes=['S32'])
                            tr.op('act', lambda e, h=h: e.copy(Sbf[:, h, :, :], S32[:, h, :, :]), reads=['S32'], writes=['Sb'])
                        else:
                            tr.op('act', lambda e, h=h: e.copy(qTzw[:, :, :, 60:64], qT[:, :, 0:64].rearrange("p a (b r) -> p a b r", r=4)),
                                  reads=['qT'], writes=['qTzw'])

                            def f(e, h=h, O=O, ob_=ob_):
                                e.matmul(PB[ob_][0:64, :], zero_b[:, 0:64], zero_b[:, :], start=True, stop=False, skip_group_check=True)
                                return e.matmul(O, sT[0:64, 0:64], v_[0:64, h * 256:h * 256 + 256], start=False, stop=False, skip_group_check=True)
                            tr.op('pe', f, reads=['sT', ('v_sb', i)] + CK, writes=[('pb', ob_)])
                            g4 = GAM[h] ** 4
                            def s0_load(bb, h=h):
                                j4 = bb % 4
                                tr.dma('sp', 'S0%d' % j4, S0[j4][:], sret[l, bb, h].rearrange("(dc p) e -> p dc e", p=128), writes=[('S0', j4)])
                            s0_load(0); s0_load(1)
                            for bb in range(16):
                                j = bb % 4
                                if bb + 2 < 16:
                                    s0_load(bb + 2)
                                tr.op('act', lambda e, j=j: e.copy(S0b[j][:], S0[j][:]), reads=[('S0', j)], writes=[('S0b', j)])
                                tr.op('act', lambda e, j=j, bb=bb, h=h: e.activation(vmk[j % 2][:, :], v_[0:64, h * 256:h * 256 + 256], AF.Copy,
                                                                               scale=mbm[:, bb:bb + 1]),
                                      reads=[('v_sb', i)] + CK, writes=[('vmk', j % 2)])

                                def f(e, j=j, bb=bb, O=O):
                                    for dc in range(2):
                                        ins = e.matmul(O, qTzw[:, dc, bb, 60 - 4 * bb:124 - 4 * bb], S0b[j][:, dc, :], start=False, stop=False,
                                                       skip_group_check=True)
                                    return ins
                                tr.op('pe', f, reads=['qTzw', ('S0b', j)], writes=[('pb', ob_)])

                                def f(e, j=j, h=h):
                                    for dc in range(2):
                                        ins = e.matmul(PB[j % 2][:, 256 * dc:256 * dc + 256],
                                                       q_[0:64, 1024 + h * 256 + dc * 128:1024 + h * 256 + dc * 128 + 128], vmk[j % 2][0:64, :],
                                                       start=True, stop=True, skip_group_check=True)
                                    return ins
                                tr.op('pe', f, reads=[('qk_sb', i), ('vmk', j % 2)], writes=[('pb', j % 2)])
                                tr.op('dve', lambda e, j=j: e.tensor_tensor(S0[j][:], S0[j][:], PB[j % 2][:, :].rearrange("p (a b) -> p a b", b=256), ALU.add),
                                      reads=[('pb', j % 2), ('S0', j)], writes=[('S0', j)])
                                tr.op('dve', lambda e, j=j, g4=g4: e.tensor_scalar(S0[j][:], S0[j][:], g4, 0.0, ALU.mult, ALU.add),
                                      reads=[('S0', j)], writes=[('S0', j)])
                                tr.dma('pool', 'S0%d' % j, o_rets[l, bb, h].rearrange("(dc p) e -> p dc e", p=128), S0[j][:],
                                       reads=[('S0', j)], writes=[('o_rets', l, bb, h)])
                            head_finish(rows, h, O, ('pb', ob_), i)
                    if not smp:
                        tr.dma('sp', 'ol%d' % i, OL[t * 128:(t + 1) * 128, :], olt[i][:, :], reads=[('olt', i)], writes=[('OL', t)])
                    else:
                        pooling(t, rows, i, True)
                        if dbg and l == dbg - 1:
                            tr.dma('sp', 'cat%d' % i, dbg_cat[t * 128:t * 128 + rows, :], cat[i][0:rows, :], reads=[('cat', i)], writes=[('dbgc', t)])
                        transpose_to(lambda c0, n, t=t, rows=rows: xt_dst(t, rows, c0, n), cat[i], rows, 16, [('cat', i)], xt_wk(t))

                tr.dma('sp', 'xs', ib[0:1024, :].rearrange("(h dc p) e -> p h dc e", p=128, dc=2), S32[:], reads=['S32'], writes=['ib0'])
                tr.dma('sp', 'xu', ib[1024:1088, :], Ud[NPT * 128 - 16:NPT * 128, :].rearrange("r (a e) -> (r a) e", a=4),
                       reads=uk(NPT - 1), writes=['ib1'])
                tr.collective('cc', lambda e: e.collective_compute("AllGather", ALU.bypass, replica_groups=[[2 * i_, 2 * i_ + 1] for i_ in range(ncores // 2)],
                                                                  ins=[ib_t.ap().opt()], outs=[ob_t.ap().opt()]),
                              reads=['ib0', 'ib1'], writes=['ob'])
                tr.dma('sp', 'xl', Sh32[:], ob[0:1024, :].rearrange("(h dc p) e -> p h dc e", p=128, dc=2), reads=['ob'], writes=['Sh32'])
                tr.dma('pool', 'xp', ubp[112:128, :], ob[1024:1088, :].rearrange("(r a) e -> r (a e)", a=4), reads=['ob', 'ubp'], writes=['ubp'])
                tr.op('act', lambda e: e.copy(Shb[:], Sh32[:]), reads=['Sh32'], writes=['Shb'])
                for h in range(4):
                    tr.op('dve', lambda e, h=h: e.scalar_tensor_tensor(S32[:, h, :, :], Sh32[:, h, :, :], csc[:, NPT, h:h + 1], S32[:, h, :, :],
                                                                       ALU.mult, ALU.add),
                          reads=['Sh32', 'S32'] + CK, writes=['S32'])
                tr.dma('sp', 'S32o', o_retp[l].rearrange("h (dc p) e -> p h dc e", p=128), S32[:], reads=['S32'], writes=[('o_retp', l)])

                def b_load(t):
                    i = t % 2
                    tr.dma('sp', 'qk%d' % i, qk_sb[i][:, 0:1024], QK[t * 128:(t + 1) * 128, 0:1024], reads=qkk(t), writes=[('qk_sb', i)])
                    tr.dma('sp', 'ol%d' % i, olt[i][:, :], OL[t * 128:(t + 1) * 128, :], reads=[('OL', t)], writes=[('olt', i)])
                    tr.dma('sp', 'g%d' % i, g_sb[i][:, :], Gd[t * 128:(t + 1) * 128, :], reads=gk(t), writes=[('g_sb', i)])
                    tr.dma('pool', 'ub%d' % (t % 3), ub[t % 3][:, :], Ud[t * 128:(t + 1) * 128, :], reads=uk(t), writes=[('ub', t % 3)])
                b_load(0)
                for t in range(NPT):
                    i = t % 2
                    if t + 1 < NPT:
                        b_load(t + 1)
                    q_ = qk_sb[i]

                    def f(e):
                        pq = bfv(PB[3])
                        for c in range(8):
                            ins = e.transpose(pq[:, c * 128:(c + 1) * 128], q_[:, c * 128:(c + 1) * 128], identb[:, :])
                        return ins
                    tr.op('pe', f, reads=[('qk_sb', i)] + CK, writes=[('pb', 3)])
                    tr.op('act', lambda e: e.copy(qT4[:, :, :], bfv(PB[3]).rearrange("p (a b) -> p a b", b=128)), reads=[('pb', 3)], writes=['qT4'])

                    def f(e):
                        for h in range(4):
                            for dc in range(2):
                                ins = e.matmul(PB[4 + h // 2][:, (h % 2) * 256:(h % 2) * 256 + 256], qT4[:, 2 * h + dc, :], Shb[:, h, dc, :],
                                               start=(dc == 0), stop=(dc == 1), skip_group_check=True)
                        return ins
                    tr.op('pe', f, reads=['qT4', 'Shb'], writes=[('pb', 4), ('pb', 5)])
                    s_ = st2
                    for h in range(4):
                        tr.op('dve', lambda e, h=h, t=t: e.scalar_tensor_tensor(osb4[:, h, :], OC4[:, h, :], csc[:, t, h:h + 1],
                                                                            olt[i][:, h * 256:h * 256 + 256], ALU.mult, ALU.add),
                              reads=[('pb', 4), ('pb', 5), ('olt', i)] + CK, writes=[('osb4', h)])
                        tr.op('act', lambda e, h=h: e.activation(junk[:, :], osb4[:, h, :], AF.Square, accum_out=s_[:, h:h + 1]),
                              reads=[('osb4', h)], writes=['junk', ('ss', h)])
                    tr.op('dve', lambda e: e.tensor_scalar(s_[:, 4:8], s_[:, 0:4], 1.0 / 256, EPS, ALU.mult, ALU.add),
                          reads=[('ss', h) for h in range(4)], writes=['ss2'])
                    tr.op('act', lambda e: e.activation(s_[:, 4:8], s_[:, 4:8], AF.Sqrt), reads=['ss2'], writes=['ss2'])
                    tr.op('dve', lambda e: e.reciprocal(s_[:, 4:8], s_[:, 4:8]), reads=['ss2'], writes=['ss2'])
                    for h in range(4):
                        tr.op('dve', lambda e, h=h: e.scalar_tensor_tensor(on4[:, h * 256:(h + 1) * 256], osb4[:, h, :], s_[:, 4 + h:5 + h],
                                                                       wrow2[:, h * 256:h * 256 + 256], ALU.mult, ALU.mult),
                              reads=[('osb4', h), 'ss2', 'wrow2'], writes=[('on4', h)])
                    tr.op('dve', lambda e: e.tensor_tensor(cat[i][:, 0:1024], on4[:, :], g_sb[i][:, :], ALU.mult),
                          reads=[('on4', h) for h in range(4)] + [('g_sb', i)], writes=[('cat', i)])
                    pooling(t, 128, i, False)
                    if dbg and l == dbg - 1:
                        tr.dma('sp', 'cat%d' % i, dbg_cat[t * 128:(t + 1) * 128, :], cat[i][:, :], reads=[('cat', i)], writes=[('dbgc', t)])
                    transpose_to(lambda c0, n, t=t: xt_dst(t, 128, c0, n), cat[i], 128, 16, [('cat', i)], xt_wk(t))
            if stop_after == 'S3':
                break

            linear_resid(w_out[l])
            if stop_after == 'S4':
                break

            with Stage() as S:
                MT = S([128, 16, 256], BF16)
                Vb = S([128, 2, D], BF16); KT = S([128, 16, 256], BF16)
                with Stage() as S2:
                    Kb = S2([128, 2, D], BF16)
                    norm_stage(mem, 2, lambda t: 128, normw[l * 4 + 3], lambda t, rows, c0, n: MT[:, c0:c0 + n, t * 128:t * 128 + rows],
                               lambda t: [('MT', t)], lambda t: [])

                    def mk_ep(dst, bcopy, nm):
                        def ep(t, rows, nb, b):
                            i = nxt('ev', 3)
                            tr.op('act', lambda e: e.copy(ev[i][0:rows, :], PB[b][0:rows, :]), reads=[('pb', b)], writes=[('ev', i)])
                            tr.op('dve', lambda e: e.tensor_copy(bcopy[:, t, nb * 512:(nb + 1) * 512], ev[i][0:rows, :]),
                                  reads=[('ev', i)], writes=[(nm, t)])
                            tr.dma('sp', 'ev%d' % i, dst[l][t * 128:t * 128 + rows, nb * 512:(nb + 1) * 512], ev[i][0:rows, :],
                                   reads=[('ev', i)], writes=[('omk', nm, l, t, nb)])
                        return ep
                    msrc = lambda t, rows, kc: MT[:, kc, t * 128:t * 128 + rows]
                    linear_tok(2, lambda t: 128, w_mk[l], D, mk_ep(o_mk, Kb, 'Kb'), src_fn=msrc, key='MT')
                    linear_tok(2, lambda t: 128, w_mv[l], D, mk_ep(o_mv, Vb, 'Vb'), src_fn=msrc, key='MT')
                    for mc in range(2):
                        transpose_to(lambda c0, n, mc=mc: KT[:, c0:c0 + n, mc * 128:(mc + 1) * 128], Kb[:, mc, :], 128, 16,
                                     [('Kb', mc)], [('KT', mc)])
                norm_stage(X, NT, trows, normw[l * 4 + 1], xt_dst, xt_wk, xkeys)

                def xq_ep(t, rows, nb, b):
                    i = nxt('ev', 3)
                    tr.op('act', lambda e: e.copy(evb[i][0:rows, :], PB[b][0:rows, :]), reads=[('pb', b)], writes=[('evb', i)])
                    tr.dma('sp', 'evb%d' % i, XQ[t * 128:t * 128 + rows, nb * 512:(nb + 1) * 512], evb[i][0:rows, :],
                           reads=[('evb', i)], writes=[('XQ', t, nb)])
                linear_tok(NT, trows, w_xq[l], D, xq_ep)

                qx = [S([128, D], BF16) for _ in range(2)]
                aqT2 = [S([128, 16, 128], BF16) for _ in range(2)]
                pe2 = [S([128, 4, 256]) for _ in range(2)]; pn2 = [S([128, 4 * 256], BF16) for _ in range(2)]
                pTs2 = [S([128, 8, 128], BF16) for _ in range(2)]
                st3 = [S([128, 16]) for _ in range(2)]
                attn = [S([128, D], BF16) for _ in range(2)]
                kc_b = [S([128, 2, D], BF16) for _ in range(2)]; vc_b = [S([128, 2, D], BF16) for _ in range(2)]
                KTs = [S([128, 16, 256], BF16) for _ in range(2)]
                sc = 512.0 ** -0.5
                SC4 = PSALL[:, 4:6, :].rearrange("p b (h m) -> p (b h) m", m=256)

                def attn_core(rows, ktile, kkeys, vtile, vkeys, maskcol, accumulate, aqT, aqk, z):
                    pe_ = pe2[z]; pn = pn2[z]; pTs = pTs2[z]
                    def f(e):
                        for h in range(4):
                            for dc in range(4):
                                ins = e.matmul(PB[4 + h // 2][0:rows, (h % 2) * 256:(h % 2) * 256 + 256], aqT[:, 4 * h + dc, 0:rows],
                                               ktile[:, 4 * h + dc, :], start=(dc == 0), stop=(dc == 3), skip_group_check=True)
                        return ins
                    tr.op('pe', f, reads=[aqk] + kkeys, writes=[('pb', 4), ('pb', 5)])
                    s_ = st3[z][:, 0:8]
                    tr.op('dve', lambda e: e.reduce_max(s_[0:rows, 0:4], SC4[0:rows], axis=AX.X), reads=[('pb', 4), ('pb', 5)], writes=[('st3', z)])
                    tr.op('dve', lambda e: e.tensor_scalar(s_[0:rows, 4:8], s_[0:rows, 0:4], -sc, 0.0, ALU.mult, ALU.add),
                          reads=[('st3', z)], writes=[('st3', z)])
                    s2 = st3[z][:, 8:16]
                    for h in range(4):
                        tr.op('act', lambda e, h=h: e.activation(pe_[0:rows, h, :], SC4[0:rows, h, :], AF.Exp, bias=s_[0:rows, 4 + h:5 + h], scale=sc,
                                                                 accum_out=s2[0:rows, h:h + 1]),
                              reads=[('pb', 4), ('pb', 5), ('st3', z)], writes=[('pe_', z, h), ('st0', z, h)])
                    tr.op('dve', lambda e: e.reciprocal(s2[0:rows, 4:8], s2[0:rows, 0:4]), reads=[('st0', z, h) for h in range(4)], writes=[('rinv', z)])
                    if maskcol is not None:
                        tr.op('dve', lambda e: e.tensor_scalar(s2[0:rows, 4:8], s2[0:rows, 4:8], maskcol, 0.0, ALU.mult, ALU.add),
                              reads=[('rinv', z)] + CK, writes=[('rinv', z)])
                    for h in range(4):
                        tr.op('dve', lambda e, h=h: e.tensor_scalar(pn[0:rows, h * 256:(h + 1) * 256], pe_[0:rows, h, :], s2[0:rows, 4 + h:5 + h], 0.0,
                                                                  ALU.mult, ALU.add),
                              reads=[('pe_', z, h), ('rinv', z)], writes=[('pn', z)])
                    transpose_to(lambda c0, n: pTs[:, c0:c0 + n, 0:rows], pn, rows, 8, [('pn', z)], [('pTs', z)])

                    def f(e):
                        for h in range(4):
                            for mc in range(2):
                                if accumulate:
                                    ins = e.matmul(PB[h][0:rows, :], pTs[:, 2 * h + mc, 0:rows], vtile[:, mc, h * 512:(h + 1) * 512],
                                                   start=False, stop=False, skip_group_check=True)
                                else:
                                    ins = e.matmul(PB[h][0:rows, :], pTs[:, 2 * h + mc, 0:rows], vtile[:, mc, h * 512:(h + 1) * 512],
                                                   start=(mc == 0), stop=(mc == 1))
                        return ins
                    tr.op('pe', f, reads=[('pTs', z)] + vkeys, writes=[('pb', h) for h in range(4)])

                xqk = lambda t: [('XQ', t, nb) for nb in range(4)]

                def at_load(t):
                    rows = trows(t)
                    tr.dma('sp', 'qx%d' % (t % 2), qx[t % 2][0:rows, :], XQ[t * 128:t * 128 + rows, :], reads=xqk(t), writes=[('qx', t % 2)])

                def kv_load(bb):
                    j = bb % 2
                    tr.dma('pool', 'kcb%d' % j, kc_b[j][:], ck[l, bb].rearrange("(mc p) n -> p mc n", p=128), writes=[('kcb', j, 0), ('kcb', j, 1)])
                    tr.dma('pool', 'vcb%d' % j, vc_b[j][:], cv[l, bb].rearrange("(mc p) n -> p mc n", p=128), writes=[('vcb', j)])
                at_load(0)
                for t in range(NT):
                    rows = trows(t); i = t % 2
                    if t + 1 < NT:
                        at_load(t + 1)
                    if t == NPT - 1:
                        kv_load(0)
                    aqT = aqT2[i]
                    transpose_to(lambda c0, n, aqT=aqT: aqT[:, c0:c0 + n, 0:rows], qx[i], rows, 16, [('qx', i)], [('aqT', i)])
                    if t < NPT:
                        attn_core(rows, KT, [('KT', 0), ('KT', 1)], Vb, [('Vb', 0), ('Vb', 1)], None, False, aqT, ('aqT', i), i)
                    else:
                        def f(e):
                            for h in range(4):
                                ins = e.matmul(PB[h][0:64, :], zero_b[:, 0:64], zero_b[:, :], start=True, stop=False, skip_group_check=True)
                            return ins
                        tr.op('pe', f, reads=CK, writes=[('pb', h) for h in range(4)])
                        for bb in range(16):
                            j = bb % 2
                            if bb + 1 < 16:
                                kv_load(bb + 1)
                            for mc in range(2):
                                transpose_to(lambda c0, n, mc=mc, j=j: KTs[j][:, c0:c0 + n, mc * 128:(mc + 1) * 128], kc_b[j][:, mc, :], 128, 16,
                                             [('kcb', j, mc)], [('KTs', j, mc)])
                            attn_core(64, KTs[j], [('KTs', j, 0), ('KTs', j, 1)], vc_b[j], [('vcb', j)], mbm[:, bb:bb + 1], True, aqT, ('aqT', i), j)
                    tr.op('act', lambda e, rows=rows, i=i: e.copy(attn[i][0:rows, :], PSALL[0:rows, 0:4, :].rearrange("p b n -> p (b n)")),
                          reads=[('pb', h) for h in range(4)], writes=[('attn', i)])
                    transpose_to(lambda c0, n, t=t, rows=rows: xt_dst(t, rows, c0, n), attn[i], rows, 16, [('attn', i)], xt_wk(t))
            linear_resid(w_xo[l])
            if stop_after == 'S9':
                break

            norm_stage(X, NT, trows, normw[l * 4 + 2], xt_dst, xt_wk, xkeys)
            with Stage() as S:
                XT2 = S([128, 16, TOK], BF16)
                rq = [S([128, 512]) for _ in range(2)]
                for kq in range(4):
                    for nbq in range(4):
                        nb = kq * 4 + nbq
                        s = wload(w_up[l][:, nb * 512:(nb + 1) * 512], 16)
                        for mi in range(4):
                            fc = nbq * 4 + mi
                            for (c0, w) in TG:
                                b = nxt('pb', 4)
                                tl = list(range(c0 // 128, (c0 + w + 127) // 128))

                                def f(e, s=s, mi=mi, c0=c0, w=w, b=b):
                                    for kc in range(16):
                                        ins = e.matmul(PB[b][:, 0:w], wbuf[:, s, kc, mi * 128:(mi + 1) * 128], XT[:, kc, c0:c0 + w],
                                                       start=(kc == 0), stop=(kc == 15))
                                    return ins
                                tr.op('pe', f, reads=[('XT', t) for t in tl] + [('w', s)], writes=[('pb', b)])
                                i = nxt('rq', 2)
                                tr.op('act', lambda e, i=i, b=b, w=w: e.activation(rq[i][:, 0:w], PB[b][:, 0:w], AF.Relu),
                                      reads=[('pb', b)], writes=[('rq', i)])
                                tr.op('dve', lambda e, i=i, w=w, fc=fc, c0=c0: e.tensor_tensor(XT2[:, fc, c0:c0 + w], rq[i][:, 0:w], rq[i][:, 0:w], ALU.mult),
                                      reads=[('rq', i)], writes=[('XT2', t) for t in tl])
                    linear_tok(NT, trows, w_down[l][kq * 2048:(kq + 1) * 2048, :], D, resid_epilogue, prefetch=resid_prefetch,
                               src_fn=lambda t, rows, kc: XT2[:, kc, t * 128:t * 128 + rows], key='XT2')

        if dbg:
            tr.dma('sp', 'dbgx', dbg_x, X, reads=[k for t in range(NT) for k in xkeys(t)], writes=['dbgx'])
        if stop_after is None:
            norm_stage(X, NT, trows, normw[8], None, None, xkeys,
                       out_dram=lambda t, rows: (y_p[t * 128:t * 128 + rows, :] if t < NPT else y_s[:, :]))
        tr.finish()
    return nc


def make_inputs(c, A, consts):
    b, half = c // 2, c % 2
    m = {
        "xp": A["x_prompt"][b, half * NPT * 128:(half + 1) * NPT * 128], "xs": A["x_sample"][16 * c:16 * c + 16].reshape(64, D),
        "mem": A["mem_prompt"][b],
        "sret": A["state_ret"][:, 16 * c:16 * c + 16], "spool": A["state_pool"][:, 16 * c:16 * c + 16],
        "ck": A["cache_mem_k"][:, 16 * c:16 * c + 16].reshape(2, 16, 256, D),
        "cv": A["cache_mem_v"][:, 16 * c:16 * c + 16].reshape(2, 16, 256, D),
        "w_in": A["w_in"], "pool_w": A["pool_w"], "w_out": A["w_out"], "w_xq": A["w_xq"], "w_mk": A["w_mk"],
        "w_mv": A["w_mv"], "w_xo": A["w_xo"], "w_up": A["w_up"], "w_down": A["w_down"],
        "normw": A["normw"], "retw": A["ret_norm_w"], "pscale": A["pool_scale"],
    }
    m.update(consts[half])
    return {k: np.ascontiguousarray(v) for k, v in m.items()}


def kernel(**inputs):
    A = {k: np.asarray(v, dtype=np.float32) for k, v in inputs.items()}
    rows = []
    for l in range(2):
        rows += [A["attn_norm_w"][l], A["xattn_norm_w"][l], A["mlp_norm_w"][l], A["mem_norm_w"][l]]
    rows.append(A["final_norm_w"])
    A["normw"] = np.stack(rows)
    consts = [host_consts(NPT, half * NPT * 128, 16384, half) for half in range(2)]
    nc = build()
    in_maps = [make_inputs(c, A, consts) for c in range(8)]
    res = run_bass_kernel_spmd(nc, in_maps, core_ids=list(range(8)))
    R = res.results
    y_p = np.stack([np.concatenate([R[2 * b]["y_p"], R[2 * b + 1]["y_p"]], axis=0) for b in range(4)])
    y_s = np.concatenate([R[c]["y_s"].reshape(16, 4, D) for c in range(8)], axis=0)
    retp = np.stack([R[2 * b + 1]["o_retp"] for b in range(4)], axis=1)
    bufp = np.stack([R[2 * b + 1]["o_bufp"] for b in range(4)], axis=1)
    mk = np.stack([R[2 * b]["o_mk"].reshape(2, 256, 4, 512) for b in range(4)], axis=1)
    mv = np.stack([R[2 * b]["o_mv"].reshape(2, 256, 4, 512) for b in range(4)], axis=1)
    rets = np.concatenate([R[c]["o_rets"] for c in range(8)], axis=1)
    bufs = np.concatenate([R[c]["o_bufs"] for c in range(8)], axis=1)
    f = lambda a: np.ascontiguousarray(a, dtype=np.float32)
    return (f(y_p), f(y_s), f(retp), f(bufp), f(mk), f(mv), f(rets), f(bufs))
```

```python
import contextlib
import numpy as np
import ml_dtypes
import concourse.bass as bass
import concourse.mybir as mybir
from concourse.bass_utils import run_bass_kernel_spmd

F32 = mybir.dt.float32
BF16 = mybir.dt.bfloat16
ALU = mybir.AluOpType
AF = mybir.ActivationFunctionType
AX = mybir.AxisListType

D = 2048
NPT = 8
NT = NPT + 1
TOK = NPT * 128 + 64
SROW = NPT * 128
EPS = 1e-6
GAM = [1.0 - 2.0 ** (-5 - h) for h in range(4)]
WINS = (2, 4, 8, 16)


def trows(t):
    return 128 if t < NPT else 64


class Tr:
    def __init__(self, nc, es):
        self.nc = nc
        self.eng = {'pe': nc.tensor, 'act': nc.scalar, 'dve': nc.vector, 'pool': nc.gpsimd, 'sp': nc.sync}
        self.sem = {e: es.enter_context(nc.semaphore('s_' + e)) for e in self.eng}
        self.cnt = {e: 0 for e in self.eng}
        self.es = es
        self.dsem = {}
        self.dcnt = {}
        self.waited = {e: {} for e in self.eng}
        self.lw = {}
        self.rd = {}

    def _wait(self, e, toks):
        need = {}
        for tk in toks:
            if tk is None:
                continue
            sk, val = tk
            if sk == e and e == 'pe':
                continue
            if need.get(sk, 0) < val:
                need[sk] = val
        for sk, val in need.items():
            if self.waited[e].get(sk, 0) >= val:
                continue
            self.waited[e][sk] = val
            s = self.sem[sk] if sk in self.sem else self.dsem[sk]
            self.eng[e].wait_ge(s, val)

    def _deps(self, reads, writes):
        toks = []
        for k in reads:
            toks.append(self.lw.get(k))
        for k in writes:
            toks.append(self.lw.get(k))
            toks.extend(self.rd.get(k, ()))
        return toks

    def _upd(self, tok, reads, writes):
        for k in reads:
            self.rd.setdefault(k, []).append(tok)
        for k in writes:
            self.lw[k] = tok
            self.rd[k] = []

    def op(self, e, fn, reads=(), writes=()):
        self._wait(e, self._deps(reads, writes))
        ins = fn(self.eng[e])
        self.cnt[e] += 1
        ins.then_inc(self.sem[e], 1)
        self._upd((e, self.cnt[e]), reads, writes)

    def dma(self, q, cls, out, in_, reads=(), writes=()):
        if cls not in self.dsem:
            self.dsem[cls] = self.es.enter_context(self.nc.semaphore('d_' + cls))
            self.dcnt[cls] = 0
        self._wait(q, self._deps(reads, writes))
        ins = self.eng[q].dma_start(out=out, in_=in_)
        self.dcnt[cls] += 16
        ins.then_inc(self.dsem[cls], 16)
        self._upd((cls, self.dcnt[cls]), reads, writes)

    def barrier(self):
        toks = [(c, v) for c, v in self.dcnt.items() if not (len(c) == 2 and c[0] == 'w' and c[1].isdigit())] + [(e, v) for e, v in self.cnt.items() if v > 0]
        for e in self.eng:
            self._wait(e, [tk for tk in toks if tk[0] != e])

    def collective(self, cls, fn, reads=(), writes=()):
        if cls not in self.dsem:
            self.dsem[cls] = self.es.enter_context(self.nc.semaphore('d_' + cls))
            self.dcnt[cls] = 0
        self._wait('pool', self._deps(reads, writes))
        ins = fn(self.eng['pool'])
        self.dcnt[cls] += 1
        ins.then_inc(self.dsem[cls], 1)
        self._upd((cls, self.dcnt[cls]), reads, writes)

    def finish(self):
        toks = [(c, v) for c, v in self.dcnt.items()] + [(e, v) for e, v in self.cnt.items() if e != 'sp' and v > 0]
        self._wait('sp', toks)


def host_consts(npt, pos0, spos0, hseq=0):
    nt = npt + 1
    half = 128
    inv = (np.float32(10000.0) ** (-(np.arange(half, dtype=np.float32) / np.float32(half)))).astype(np.float32)
    pos = np.zeros((128, nt), np.float32)
    il = np.zeros((128, nt), np.float64)
    for t in range(npt):
        pos[:, t] = pos0 + t * 128 + np.arange(128)
        il[:, t] = np.arange(128)
    pos[:, npt] = spos0 + (np.arange(128) % 4)
    il[:, npt] = np.arange(128) % 4
    ang = (pos[:, :, None] * inv[None, None, :]).astype(np.float32)
    cos = np.cos(ang).astype(np.float32); sin = np.sin(ang).astype(np.float32)
    dqk = np.zeros((128, nt, 8), np.float32)
    for h in range(4):
        g = np.float64(np.float32(np.log1p(np.float32(-2.0 ** (-5 - h)))))
        dqk[:, :, h] = np.exp(g * (il + 1.0))
        dqk[:, :, 4 + h] = np.exp(-g * (il + 1.0)) / 16.0
    j = np.arange(128)
    maskT = (j[None, :] >= j[:, None]).astype(np.float32)
    js = np.arange(64)
    maskS = ((js[None, :] >= js[:, None]) & (js[None, :] // 4 == js[:, None] // 4)).astype(np.float32)
    mb = (js[:, None] // 4 == np.arange(16)[None, :]).astype(np.float32)
    pA = np.zeros((128, 4, 4, 128), np.float32)
    pAs = np.zeros((64, 4, 64), np.float32)
    pBs = np.zeros((120, 2, 4, 64), np.float32)
    for g, w in enumerate(WINS):
        for t in range(128):
            for s in range(max(0, t - w + 1), t + 1):
                pA[s, g, 0, t] += 1.0 / min(t + 1, w)
                pA[s, g, 1, t] += 1.0 / w
            pA[t, g, 0, t] -= 1.0
            pA[t, g, 1, t] -= 1.0
            for sl in range(128):
                if sl - 128 >= t - w + 1:
                    pA[sl, g, 2, t] = 1.0 / w
        if hseq == 1:
            pA[:, g, 0, :] = pA[:, g, 1, :]
            pA[:, g, 3, :] = pA[:, g, 2, :]
        for b in range(16):
            for r in range(4):
                t = 4 * b + r
                for r2 in range(0, r + 1):
                    if r - r2 < w:
                        pAs[4 * b + r2, g, t] += 1.0 / w
                pAs[t, g, t] -= 1.0
                for k in range(15):
                    if k >= 16 + r - w:
                        pBs[(b % 8) * 15 + k, b // 8, g, t] = 1.0 / w
    bf = ml_dtypes.bfloat16
    csc = np.zeros((128, npt + 1, 4), np.float32)
    for h in range(4):
        g_ = np.float64(np.float32(np.log1p(np.float32(-2.0 ** (-5 - h)))))
        for t in range(npt + 1):
            csc[:, t, h] = hseq * np.exp(g_ * 128.0 * t)
    return {"c_csc": csc, "c_cos": cos, "c_sin": sin, "c_dqk": dqk, "c_identb": np.eye(128, dtype=np.float32).astype(bf),
            "c_maskT": maskT, "c_maskS": maskS, "c_mb": mb, "c_pA": pA.astype(bf), "c_pAs": pAs.astype(bf),
            "c_pBs": pBs.astype(bf)}


def build(stop_after=None, dbg=False, ncores=8):
    nc = bass.Bass("TRN2", target_bir_lowering=False)

    def din(name, shape, dt=F32):
        return nc.dram_tensor(name, list(shape), dt, kind="ExternalInput").ap()

    def dout(name, shape):
        return nc.dram_tensor(name, list(shape), F32, kind="ExternalOutput").ap()

    def dscr(name, shape, dt=F32):
        return nc.dram_tensor(name, list(shape), dt).ap()

    xp = din("xp", [NPT * 128, D]); xs = din("xs", [64, D]); mem = din("mem", [256, D])
    sret = din("sret", [2, 16, 4, 256, 256]); spool = din("spool", [2, 16, 15, 1024])
    ck = din("ck", [2, 16, 256, D]); cv = din("cv", [2, 16, 256, D])
    w_in = din("w_in", [2, D, 5120]); pool_w = din("pool_w", [2, 4, 256, 256])
    w_out = din("w_out", [2, D, D]); w_xq = din("w_xq", [2, D, D]); w_mk = din("w_mk", [2, D, D])
    w_mv = din("w_mv", [2, D, D]); w_xo = din("w_xo", [2, D, D])
    w_up = din("w_up", [2, D, 8192]); w_down = din("w_down", [2, 8192, D])
    normw = din("normw", [9, D])
    retw = din("retw", [2, 1024]); pscale = din("pscale", [2, 1024])
    c_cos = din("c_cos", [128, NT, 128]); c_sin = din("c_sin", [128, NT, 128])
    c_dqk = din("c_dqk", [128, NT, 8])
    c_identb = din("c_identb", [128, 128], BF16)
    c_maskT = din("c_maskT", [128, 128]); c_maskS = din("c_maskS", [64, 64])
    c_mb = din("c_mb", [64, 16])
    c_pA = din("c_pA", [128, 4, 4, 128], BF16)
    c_csc = din("c_csc", [128, NT, 4])
    c_pAs = din("c_pAs", [64, 4, 64], BF16)
    c_pBs = din("c_pBs", [120, 2, 4, 64], BF16)

    y_p = dout("y_p", [NPT * 128, D]); y_s = dout("y_s", [64, D])
    o_retp = dout("o_retp", [2, 4, 256, 256]); o_bufp = dout("o_bufp", [2, 15, 1024])
    o_mk = dout("o_mk", [2, 256, D]); o_mv = dout("o_mv", [2, 256, D])
    o_rets = dout("o_rets", [2, 16, 4, 256, 256]); o_bufs = dout("o_bufs", [2, 16, 15, 1024])
    dbg_cat = nc.dram_tensor("dbg_cat", [TOK, D], BF16, kind="ExternalOutput").ap() if dbg else None
    dbg_x = dout("dbg_x", [TOK, D]) if dbg else None

    X = dscr("X", [TOK, D])
    QK = dscr("QK", [TOK, 2048], BF16); Vd = dscr("Vd", [TOK, 1024], BF16); Gd = dscr("Gd", [TOK, 1024], BF16)
    Ud = dscr("Ud", [TOK, 1024]); XQ = dscr("XQ", [TOK, D], BF16)
    ATd = dscr("ATd", [8192, TOK], BF16)
    OL = dscr("OL", [NPT * 128, 1024])
    ib_t = nc.dram_tensor("xch_in", [1088, 256], F32)
    ob_t = nc.dram_tensor("xch_out", [2176, 256], F32)
    ib = ib_t.ap(); ob = ob_t.ap()

    TG = [(c0, min(512, TOK - c0)) for c0 in range(0, TOK, 512)]

    es = contextlib.ExitStack()
    with es:
        tr = Tr(nc, es)
        uid = [0]

        def sbx(stack, shape, dt=F32):
            uid[0] += 1
            return stack.enter_context(nc.sbuf_tensor(f"t{uid[0]}", list(shape), dt))

        sb = lambda shape, dt=F32: sbx(es, shape, dt)

        class Stage:
            def __enter__(self):
                self.st = contextlib.ExitStack()
                self.st.__enter__()
                return lambda shape, dt=F32: sbx(self.st, shape, dt)

            def __exit__(self, *a):
                tr.barrier()
                self.st.__exit__(*a)
                return False

        XT = sb([128, 16, TOK], BF16)
        NW = 2
        wbuf = sb([128, NW, 16, 512], BF16)
        dqk = sb([128, NT, 8])
        identb = sb([128, 128], BF16)
        maskT = sb([128, 128]); maskS = sb([64, 64]); mbm = sb([64, 16])
        pA = sb([128, 4, 4, 128], BF16); csc = sb([128, NT, 4]); pAs = sb([64, 4, 64], BF16); pBs = sb([120, 2, 4, 64], BF16)
        wrow2 = sb([128, 1024]); wrow3 = sb([128, 1024])
        poolw = sb([128, 4, 2, 256], BF16)
        zero_b = sb([128, 512], BF16)
        ev = [sb([128, 512]) for _ in range(3)]
        evb = [sb([128, 512], BF16) for _ in range(3)]
        st = [sb([128, 8]) for _ in range(3)]
        PSALL = es.enter_context(nc.psum_tensor("psall", [128, 8, 512], F32))
        PB = [PSALL[:, i, :] for i in range(8)]

        def bfv(bank):
            return bank[:, :].bitcast(BF16)

        for (dst, src) in [(csc, c_csc), (dqk, c_dqk), (identb, c_identb), (maskT, c_maskT),
                           (maskS, c_maskS), (mbm, c_mb), (pA, c_pA), (pAs, c_pAs), (pBs, c_pBs)]:
            tr.dma('sp', 'c_' + dst.name, dst[:], src, writes=[('c', dst.name)])
        tr.op('dve', lambda e: e.memset(zero_b[:], 0.0), writes=['zero_b'])
        CK = [('c', t_.name) for t_ in (csc, dqk, identb, maskT, maskS, mbm, pA, pAs, pBs)] + ['zero_b']

        xkeys = lambda t: [('X', t, nb) for nb in range(4)]
        tr.dma('sp', 'xinit0', X[0:NPT * 128, :], xp, writes=[k for t in range(NPT) for k in xkeys(t)])
        tr.dma('sp', 'xinit1', X[SROW:TOK, :], xs, writes=xkeys(NPT))

        wstate = {'n': 0}

        def wload(src_ap, nk):
            s = wstate['n'] % NW
            wstate['n'] += 1
            tr.dma('pool', 'w%d' % s, wbuf[:, s, 0:nk, :], src_ap.rearrange("(c p) n -> p c n", p=128), writes=[('w', s)])
            return s

        cnt = {}

        def nxt(k, n):
            v = cnt.get(k, 0) % n
            cnt[k] = cnt.get(k, 0) + 1
            return v

        def transpose_to(dst_fn, src_tile, rows, nchunks, rk, wk):
            for c0 in range(0, nchunks, 8):
                n = min(8, nchunks - c0)
                b = 6 + nxt('x', 2)
                pv = bfv(PB[b])

                def f(e, c0=c0, n=n, pv=pv):
                    for j in range(n):
                        ins = e.transpose(pv[:, j * 128:j * 128 + rows], src_tile[0:rows, (c0 + j) * 128:(c0 + j + 1) * 128],
                                          identb[0:rows, 0:rows])
                    return ins
                tr.op('pe', f, reads=list(rk) + CK, writes=[('pb', b)])
                src = pv[:, 0:n * 128].rearrange("p (j t) -> p j t", t=128)[:, :, 0:rows]
                if nxt('xe', 2) == 0:
                    tr.op('act', lambda e, src=src, c0=c0, n=n: e.copy(dst_fn(c0, n), src), reads=[('pb', b)], writes=list(wk))
                else:
                    tr.op('dve', lambda e, src=src, c0=c0, n=n: e.tensor_copy(dst_fn(c0, n), src), reads=[('pb', b)], writes=list(wk))

        def rms_rstd(src, rows, width, sidx, rk, junk):
            s = st[sidx]
            tr.op('act', lambda e: e.activation(junk[0:rows, 0:width], src, AF.Square, accum_out=s[0:rows, 0:1]),
                  reads=list(rk), writes=['junk', ('st', sidx)])
            tr.op('dve', lambda e: e.tensor_scalar(s[0:rows, 1:2], s[0:rows, 0:1], 1.0 / width, EPS, ALU.mult, ALU.add),
                  reads=[('st', sidx)], writes=[('st', sidx)])
            tr.op('act', lambda e: e.activation(s[0:rows, 2:3], s[0:rows, 1:2], AF.Sqrt), reads=[('st', sidx)], writes=[('st', sidx)])
            tr.op('dve', lambda e: e.reciprocal(s[0:rows, 3:4], s[0:rows, 2:3]), reads=[('st', sidx)], writes=[('st', sidx)])
            return s[0:rows, 3:4]

        def load_wrow(dst, src_row, key):
            tr.dma('sp', 'wr_' + key, dst[:], src_row.partition_broadcast(128), writes=[key])

        def norm_stage(src_dram, ntiles, rows_fn, wsrc, dst_fn, wk_fn, xkeyfn, out_dram=None):
            with Stage() as S:
                wrow = S([128, D])
                load_wrow(wrow, wsrc, 'wrow')
                xt_in = [S([128, D]) for _ in range(3)]
                hb = [S([128, D], BF16) for _ in range(3)]
                junk = S([128, D], BF16)
                for t in range(ntiles):
                    rows = rows_fn(t)
                    i = t % 3
                    tr.dma('sp', 'xtin%d' % i, xt_in[i][0:rows, :], src_dram[t * 128:t * 128 + rows, :], reads=xkeyfn(t), writes=[('xt_in', i)])
                    r = rms_rstd(xt_in[i][0:rows, :], rows, D, i, [('xt_in', i)], junk)
                    if out_dram is None:
                        tr.op('dve', lambda e, i=i, rows=rows, r=r: e.scalar_tensor_tensor(
                            hb[i][0:rows, :], xt_in[i][0:rows, :], r, wrow[0:rows, :], ALU.mult, ALU.mult),
                            reads=[('xt_in', i), ('st', i), 'wrow'], writes=[('hb', i)])
                        transpose_to(lambda c0, n, t=t, rows=rows: dst_fn(t, rows, c0, n), hb[i], rows, 16, [('hb', i)], wk_fn(t))
                    else:
                        tr.op('dve', lambda e, i=i, rows=rows, r=r: e.scalar_tensor_tensor(
                            xt_in[i][0:rows, :], xt_in[i][0:rows, :], r, wrow[0:rows, :], ALU.mult, ALU.mult),
                            reads=[('xt_in', i), ('st', i), 'wrow'], writes=[('xt_in', i)])
                        tr.dma('sp', 'xtin%d' % i, out_dram(t, rows), xt_in[i][0:rows, :], reads=[('xt_in', i)], writes=[('y', t)])

        xt_dst = lambda t, rows, c0, n: XT[:, c0:c0 + n, t * 128:t * 128 + rows]
        xt_wk = lambda t: [('XT', t)]

        def linear_tok(ntiles, rows_fn, W, ncols, epilogue, src_fn=None, nk=16, key='XT', prefetch=None):
            sf = src_fn or (lambda t, rows, kc: XT[:, kc, t * 128:t * 128 + rows])
            for nb in range(ncols // 512):
                s = wload(W[:, nb * 512:(nb + 1) * 512], nk)
                if prefetch:
                    prefetch(0, rows_fn(0), nb)
                for t in range(ntiles):
                    rows = rows_fn(t)
                    b = nxt('pb', 4)

                    def f(e, t=t, rows=rows, b=b, s=s):
                        for kc in range(nk):
                            ins = e.matmul(PB[b][0:rows, :], sf(t, rows, kc), wbuf[:, s, kc, :], start=(kc == 0), stop=(kc == nk - 1))
                        return ins
                    tr.op('pe', f, reads=[(key, t), ('w', s)], writes=[('pb', b)])
                    if prefetch and t + 1 < ntiles:
                        prefetch(t + 1, rows_fn(t + 1), nb)
                    epilogue(t, rows, nb, b)

        rbuf = {}

        def resid_prefetch(t, rows, nb):
            i = nxt('ev', 3)
            rbuf[(t, nb)] = i
            tr.dma('sp', 'ev%d' % i, ev[i][0:rows, :], X[t * 128:t * 128 + rows, nb * 512:(nb + 1) * 512],
                   reads=[('X', t, nb)], writes=[('ev', i)])

        def resid_epilogue(t, rows, nb, b):
            i = rbuf.pop((t, nb))
            tr.op('dve', lambda e: e.tensor_tensor(ev[i][0:rows, :], ev[i][0:rows, :], PB[b][0:rows, :], ALU.add),
                  reads=[('pb', b), ('ev', i)], writes=[('ev', i)])
            tr.dma('sp', 'ev%d' % i, X[t * 128:t * 128 + rows, nb * 512:(nb + 1) * 512], ev[i][0:rows, :],
                   reads=[('ev', i)], writes=[('X', t, nb)])

        def linear_resid(W, nk=16):
            linear_tok(NT, trows, W, D, resid_epilogue, nk=nk, prefetch=resid_prefetch)

        for l in range(2):
            tr.dma('pool', 'poolw', poolw[:], pool_w[l].rearrange("g (cc p) d -> p g cc d", p=128), writes=['poolw'])
            load_wrow(wrow2, retw[l], 'wrow2')
            load_wrow(wrow3, pscale[l], 'wrow3')

            norm_stage(X, NT, trows, normw[l * 4 + 0], xt_dst, xt_wk, xkeys)

            with Stage() as S:
                tas = [S([128, 512]) for _ in range(2)]; tbs = [S([128, 512]) for _ in range(2)]
                nsin_t = S([128, NT, 128])
                cos_t = S([128, NT, 128]); sin_t = S([128, NT, 128])
                tr.dma('sp', 'cos', cos_t[:], c_cos, writes=['cos'])
                tr.dma('sp', 'sin', sin_t[:], c_sin, writes=['sin'])
                tr.op('dve', lambda e: e.tensor_scalar(nsin_t[:], sin_t[:], -1.0, 0.0, ALU.mult, ALU.add), reads=['sin'], writes=['nsin'])

                def inproj_ep(t, rows, nb, b):
                    P = PB[b]
                    i = nxt('ev', 3)
                    if nb < 4:
                        isk = nb >= 2
                        k2 = nxt('rope', 2)
                        ta, tb = tas[k2], tbs[k2]
                        P3 = P[0:rows, :].rearrange("p (c f) -> p c f", f=128)
                        P4 = P[0:rows, :].rearrange("p (hh hf f) -> p hh hf f", hh=2, hf=2)
                        cos4 = cos_t[0:rows, t:t + 1, :].broadcast_to([rows, 4, 128])
                        sin2 = sin_t[0:rows, t:t + 1, :].broadcast_to([rows, 2, 128])
                        nsin2 = nsin_t[0:rows, t:t + 1, :].broadcast_to([rows, 2, 128])
                        ta3 = ta[0:rows, :].rearrange("p (c f) -> p c f", f=128)
                        tb4 = tb[0:rows, :].rearrange("p (hh hf f) -> p hh hf f", hh=2, hf=2)
                        tr.op('dve', lambda e: e.tensor_tensor(ta3, P3, cos4, ALU.mult), reads=[('pb', b), 'cos'] + CK, writes=[('ta', k2)])
                        tr.op('dve', lambda e: e.tensor_tensor(tb4[:, :, 0, :], P4[:, :, 1, :], nsin2, ALU.mult), reads=[('pb', b), 'nsin'], writes=[('tb', k2, 0)])
                        tr.op('dve', lambda e: e.tensor_tensor(tb4[:, :, 1, :], P4[:, :, 0, :], sin2, ALU.mult), reads=[('pb', b), 'sin'], writes=[('tb', k2, 1)])
                        tr.op('dve', lambda e: e.tensor_tensor(ta[0:rows, :], ta[0:rows, :], tb[0:rows, :], ALU.add),
                              reads=[('ta', k2), ('tb', k2, 0), ('tb', k2, 1)], writes=[('ta', k2)])
                        for hh in range(2):
                            h = (nb % 2) * 2 + hh
                            ci = (4 if isk else 0) + h
                            dcol = dqk[0:rows, t, ci:ci + 1]
                            tr.op('act', lambda e, hh=hh, dcol=dcol: e.activation(evb[i][0:rows, hh * 256:(hh + 1) * 256], ta[0:rows, hh * 256:(hh + 1) * 256],
                                                                              AF.Copy, scale=dcol),
                                  reads=[('ta', k2)] + CK, writes=[('evb', i)])
                        tr.dma('sp', 'evb%d' % i, QK[t * 128:t * 128 + rows, nb * 512:(nb + 1) * 512], evb[i][0:rows, :],
                               reads=[('evb', i)], writes=[('QK', t, nb)])
                    elif nb < 6:
                        tr.op('act', lambda e: e.copy(evb[i][0:rows, :], P[0:rows, :]), reads=[('pb', b)], writes=[('evb', i)])
                        tr.dma('sp', 'evb%d' % i, Vd[t * 128:t * 128 + rows, (nb - 4) * 512:(nb - 3) * 512], evb[i][0:rows, :],
                               reads=[('evb', i)], writes=[('V', t, nb)])
                    elif nb < 8:
                        tr.op('act', lambda e: e.activation(evb[i][0:rows, :], P[0:rows, :], AF.Silu), reads=[('pb', b)], writes=[('evb', i)])
                        tr.dma('sp', 'evb%d' % i, Gd[t * 128:t * 128 + rows, (nb - 6) * 512:(nb - 5) * 512], evb[i][0:rows, :],
                               reads=[('evb', i)], writes=[('G', t, nb)])
                    else:
                        tr.op('act', lambda e: e.copy(ev[i][0:rows, :], P[0:rows, :]), reads=[('pb', b)], writes=[('ev', i)])
                        tr.dma('sp', 'ev%d' % i, Ud[t * 128:t * 128 + rows, (nb - 8) * 512:(nb - 7) * 512], ev[i][0:rows, :],
                               reads=[('ev', i)], writes=[('U', t, nb)])

                linear_tok(NT, trows, w_in[l], 5120, inproj_ep)
            qkk = lambda t: [('QK', t, nb) for nb in range(4)]
            vk = lambda t: [('V', t, 4), ('V', t, 5)]
            gk = lambda t: [('G', t, 6), ('G', t, 7)]
            uk = lambda t: [('U', t, 8), ('U', t, 9)]

            tr.dma('sp', 'ob_a%d' % l, o_bufp[l], Ud[NPT * 128 - 15:NPT * 128, :], reads=uk(NPT - 1), writes=[('o_bufp', l)])
            tr.dma('sp', 'ob_b%d' % l, o_bufs[l][:, 0:11, :], spool[l][:, 4:15, :], writes=[('o_bufs', l, 0)])
            tr.dma('sp', 'ob_c%d' % l, o_bufs[l][:, 11:15, :], Ud[SROW:TOK, :].rearrange("(b r) n -> b r n", r=4),
                   reads=uk(NPT), writes=[('o_bufs', l, 1)])
            if stop_after == 'S2':
                break

            with Stage() as S:
                qk_sb = [S([128, 2048], BF16) for _ in range(2)]
                v_sb = [S([128, 1024], BF16) for _ in range(2)]
                g_sb = [S([128, 1024], BF16) for _ in range(2)]
                ub = [S([128, 1024], BF16) for _ in range(3)]
                ubp = S([128, 1024], BF16)
                olt = [S([128, 1024]) for _ in range(2)]
                cat = [S([128, D], BF16) for _ in range(2)]
                qT = S([128, 2, 128], BF16); kT = S([128, 2, 128], BF16); sT = S([128, 128], BF16)
                vg = S([128, 256], BF16)
                S32 = S([128, 4, 2, 256]); Sbf = S([128, 4, 2, 256], BF16)
                Sh32 = S([128, 4, 2, 256]); Shb = S([128, 4, 2, 256], BF16)
                S0 = [S([128, 2, 256]) for _ in range(4)]; S0b = [S([128, 2, 256], BF16) for _ in range(4)]
                vmk = [S([64, 256], BF16) for _ in range(2)]
                qTzw = S([128, 2, 16, 124], BF16)
                on = S([128, 256]); osb = S([128, 256]); junk = S([128, 256], BF16)
                qT4 = S([128, 8, 128], BF16); kT4 = S([128, 8, 128], BF16); sT4 = S([128, 4, 128], BF16)
                st2 = S([128, 8])
                vg4 = S([128, 4, 256], BF16); osb4 = S([128, 4, 256]); on4 = S([128, 4 * 256])
                OC4 = PSALL[:, 4:6, :].rearrange("p b (h m) -> p (b h) m", m=256)
                bufb = S([120, 2, 1024], BF16)
                pT = S([128, 8, 128], BF16)

                tr.op('dve', lambda e: e.memset(S32[:], 0.0), writes=['S32'])
                tr.op('dve', lambda e: e.memset(Sbf[:], 0.0), writes=['Sb'])
                tr.op('dve', lambda e: e.memset(qTzw[:], 0.0), writes=['qTzw'])
                tr.op('dve', lambda e: e.memset(ubp[:], 0.0), writes=['ubp'])
                tr.dma('pool', 'bufb', bufb[:], spool[l].rearrange("(k b) r n -> (b r) k n", k=2), writes=['bufb'])

                def s3_load(t):
                    rows = trows(t); i = t % 2
                    tr.dma('sp', 'qk%d' % i, qk_sb[i][0:rows, :], QK[t * 128:t * 128 + rows, :], reads=qkk(t), writes=[('qk_sb', i)])
                    tr.dma('sp', 'v%d' % i, v_sb[i][0:rows, :], Vd[t * 128:t * 128 + rows, :], reads=vk(t), writes=[('v_sb', i)])
                    if t == NPT:
                        tr.dma('sp', 'g%d' % i, g_sb[i][0:rows, :], Gd[t * 128:t * 128 + rows, :], reads=gk(t), writes=[('g_sb', i)])
                        tr.dma('pool', 'ub%d' % (t % 3), ub[t % 3][0:rows, :], Ud[t * 128:t * 128 + rows, :], reads=uk(t), writes=[('ub', t % 3)])

                def head_finish(rows, h, osrc, okey, i):
                    r = rms_rstd(osrc, rows, 256, 0, [okey], junk)
                    tr.op('dve', lambda e: e.scalar_tensor_tensor(on[0:rows, :], osrc, r, wrow2[0:rows, h * 256:h * 256 + 256], ALU.mult, ALU.mult),
                          reads=[okey, ('st', 0), 'wrow2'], writes=['on'])
                    tr.op('dve', lambda e: e.tensor_tensor(cat[i][0:rows, h * 256:h * 256 + 256], on[0:rows, :],
                                                          g_sb[i][0:rows, h * 256:h * 256 + 256], ALU.mult),
                          reads=['on', ('g_sb', i)], writes=[('cat', i)])

                def qk_transposes(q_, rows, h, i, with_k=True):
                    def f(e, h=h):
                        pv = bfv(PB[3])
                        for dc in range(2):
                            ins = e.transpose(pv[:, dc * 128:dc * 128 + rows], q_[0:rows, h * 256 + dc * 128:h * 256 + dc * 128 + 128],
                                              identb[0:rows, 0:rows])
                        if with_k:
                            for dc in range(2):
                                ins = e.transpose(pv[:, 256 + dc * 128:256 + dc * 128 + rows],
                                                  q_[0:rows, 1024 + h * 256 + dc * 128:1024 + h * 256 + dc * 128 + 128],
                                                  identb[0:rows, 0:rows])
                        return ins
                    tr.op('pe', f, reads=[('qk_sb', i)] + CK, writes=[('pb', 3)])
                    pv3 = bfv(PB[3])
                    tr.op('act', lambda e: e.copy(qT[:, :, 0:rows], pv3[:, 0:256].rearrange("p (a b) -> p a b", b=128)[:, :, 0:rows]),
                          reads=[('pb', 3)], writes=['qT'])
                    if with_k:
                        tr.op('act', lambda e: e.copy(kT[:, :, 0:rows], pv3[:, 256:512].rearrange("p (a b) -> p a b", b=128)[:, :, 0:rows]),
                              reads=[('pb', 3)], writes=['kT'])

                def pooling(t, rows, i, smp):
                    u3 = t % 3; p3 = (t - 1) % 3
                    for g in range(4):
                        def f(e, g=g):
                            for cc in range(2):
                                ch = 2 * g + cc
                                out = PB[1][:, cc * 128:cc * 128 + rows]
                                if smp:
                                    e.matmul(out, ub[u3][0:64, ch * 128:ch * 128 + 128], pAs[:, g, :], start=True, stop=False)
                                    for kk in range(2):
                                        ins = e.matmul(out, bufb[:, kk, ch * 128:ch * 128 + 128], pBs[:, kk, g, :], start=False, stop=(kk == 1))
                                elif t == 0:
                                    e.matmul(out, ub[u3][:, ch * 128:ch * 128 + 128], pA[:, g, 0, :], start=True, stop=False)
                                    ins = e.matmul(out, ubp[:, ch * 128:ch * 128 + 128], pA[:, g, 3, :], start=False, stop=True)
                                else:
                                    e.matmul(out, ub[u3][:, ch * 128:ch * 128 + 128], pA[:, g, 1, :], start=True, stop=False)
                                    ins = e.matmul(out, ub[p3][:, ch * 128:ch * 128 + 128], pA[:, g, 2, :], start=False, stop=True)
                            return ins
                        tr.op('pe', f, reads=[('ub', u3), ('ub', p3), 'bufb', 'ubp'] + CK, writes=[('pb', 1)])
                        tr.op('act', lambda e, g=g: e.copy(pT[:, 2 * g:2 * g + 2, 0:rows],
                                                           PB[1][:, 0:256].rearrange("p (a b) -> p a b", b=128)[:, :, 0:rows]),
                              reads=[('pb', 1)], writes=[('pT', g)])

                        def f(e, g=g):
                            for cc in range(2):
                                ins = e.matmul(PB[0][0:rows, 0:256], pT[:, 2 * g + cc, 0:rows], poolw[:, g, cc, :], start=(cc == 0), stop=(cc == 1),
                                               skip_group_check=True)
                            return ins
                        tr.op('pe', f, reads=[('pT', g), 'poolw'], writes=[('pb', 0)])
                        tr.op('dve', lambda e, g=g: e.tensor_tensor(cat[i][0:rows, 1024 + g * 256:1024 + g * 256 + 256], PB[0][0:rows, 0:256],
                                                                  wrow3[0:rows, g * 256:g * 256 + 256], ALU.mult),
                              reads=[('pb', 0), 'wrow3'], writes=[('cat', i)])

                def passA_prompt(t, i):
                    q_ = qk_sb[i]; v_ = v_sb[i]
                    def f(e):
                        pq = bfv(PB[3]); pk = bfv(PB[7])
                        for c in range(8):
                            e.transpose(pq[:, c * 128:(c + 1) * 128], q_[:, c * 128:(c + 1) * 128], identb[:, :])
                        for c in range(8):
                            ins = e.transpose(pk[:, c * 128:(c + 1) * 128], q_[:, 1024 + c * 128:1024 + (c + 1) * 128], identb[:, :])
                        return ins
                    tr.op('pe', f, reads=[('qk_sb', i)] + CK, writes=[('pb', 3), ('pb', 7)])
                    tr.op('act', lambda e: e.copy(qT4[:, :, :], bfv(PB[3]).rearrange("p (a b) -> p a b", b=128)), reads=[('pb', 3)], writes=['qT4'])
                    tr.op('act', lambda e: e.copy(kT4[:, :, :], bfv(PB[7]).rearrange("p (a b) -> p a b", b=128)), reads=[('pb', 7)], writes=['kT4'])

                    def f(e):
                        for h in range(4):
                            for dc in range(2):
                                ins = e.matmul(PB[2][:, h * 128:(h + 1) * 128], kT4[:, 2 * h + dc, :], qT4[:, 2 * h + dc, :],
                                               start=(dc == 0), stop=(dc == 1), skip_group_check=True)
                        return ins
                    tr.op('pe', f, reads=['qT4', 'kT4'], writes=[('pb', 2)])
                    for h in range(4):
                        tr.op('dve', lambda e, h=h: e.tensor_tensor(sT4[:, h, :], PB[2][:, h * 128:(h + 1) * 128], maskT[:, :], ALU.mult),
                              reads=[('pb', 2)] + CK, writes=[('sT4', h)])

                    def f(e):
                        for h in range(4):
                            O = PB[4 + h // 2][:, (h % 2) * 256:(h % 2) * 256 + 256]
                            e.matmul(O, sT4[:, h, :], v_[:, h * 256:h * 256 + 256], start=True, stop=False, skip_group_check=True)
                            for dc in range(2):
                                ins = e.matmul(O, qT4[:, 2 * h + dc, :], Sbf[:, h, dc, :], start=False, stop=(dc == 1), skip_group_check=True)
                        return ins
                    tr.op('pe', f, reads=[('sT4', h) for h in range(4)] + [('v_sb', i), 'qT4', 'Sb'], writes=[('pb', 4), ('pb', 5)])
                    tr.op('act', lambda e: e.copy(olt[i][:, :], PSALL[:, 4:6, :].rearrange("p b n -> p (b n)")),
                          reads=[('pb', 4), ('pb', 5)], writes=[('olt', i)])
                    for h in range(4):
                        tr.op('act', lambda e, h=h: e.activation(vg4[:, h, :], v_[:, h * 256:h * 256 + 256], AF.Copy, scale=GAM[h] ** 128),
                              reads=[('v_sb', i)], writes=[('vg4', h)])
                    for h in range(4):
                        ub_ = h % 2

                        def f(e, h=h, ub_=ub_):
                            for dc in range(2):
                                ins = e.matmul(PB[ub_][:, 256 * dc:256 * dc + 256],
                                               q_[:, 1024 + h * 256 + dc * 128:1024 + h * 256 + dc * 128 + 128], vg4[:, h, :],
                                               start=True, stop=True, skip_group_check=True)
                            return ins
                        tr.op('pe', f, reads=[('qk_sb', i), ('vg4', h)], writes=[('pb', ub_)])
                        tr.op('dve', lambda e, h=h, ub_=ub_: e.scalar_tensor_tensor(
                            S32[:, h, :, :], S32[:, h, :, :], GAM[h] ** 128, PB[ub_][:, :].rearrange("p (a b) -> p a b", b=256), ALU.mult, ALU.add),
                            reads=[('pb', ub_), 'S32'], writes=['S32'])
                    tr.op('act', lambda e: e.copy(Sbf[:], S32[:]), reads=['S32'], writes=['Sb'])
                    tr.dma('sp', 'ol%d' % i, OL[t * 128:(t + 1) * 128, :], olt[i][:, :], reads=[('olt', i)], writes=[('OL', t)])

                s3_load(0)
                for t in range(NT):
                    rows = trows(t); i = t % 2
                    if t + 1 < NT:
                        s3_load(t + 1)
                    q_ = qk_sb[i]; v_ = v_sb[i]
                    smp = (t == NPT)
                    if not smp:
                        passA_prompt(t, i)
                        continue
                    for h in range(4):
                        qk_transposes(q_, rows, h, i)

                        def f(e):
                            for dc in range(2):
                                ins = e.matmul(PB[2][0:rows, 0:rows], kT[:, dc, 0:rows], qT[:, dc, 0:rows], start=(dc == 0), stop=(dc == 1))
                            return ins
                        tr.op('pe', f, reads=['qT', 'kT'], writes=[('pb', 2)])
                        msk = maskS if smp else maskT
                        tr.op('dve', lambda e, msk=msk: e.tensor_tensor(sT[0:rows, 0:rows], PB[2][0:rows, 0:rows], msk[0:rows, 0:rows], ALU.mult),
                              reads=[('pb', 2)] + CK, writes=['sT'])
                        ob_ = 4 + (h % 2)
                        O = PB[ob_][0:rows, 0:256]
                        if not smp:
                            def f(e, h=h, O=O):
                                e.matmul(O, sT[0:rows, 0:rows], v_[0:rows, h * 256:h * 256 + 256], start=True, stop=False)
                                for dc in range(2):
                                    ins = e.matmul(O, qT[:, dc, 0:rows], Sbf[:, h, dc, :], start=False, stop=(dc == 1))
                                return ins
                            tr.op('pe', f, reads=['sT', ('v_sb', i), 'qT', 'Sb'], writes=[('pb', ob_)])
                            tr.op('act', lambda e, h=h, O=O: e.copy(olt[i][:, h * 256:h * 256 + 256], O), reads=[('pb', ob_)], writes=[('olt', i)])
                            g128 = GAM[h] ** 128
                            tr.op('act', lambda e, h=h, g128=g128: e.activation(vg[0:rows, :], v_[0:rows, h * 256:h * 256 + 256], AF.Copy, scale=g128),
                                  reads=[('v_sb', i)], writes=['vg'])

                            def f(e, h=h):
                                for dc in range(2):
                                    ins = e.matmul(PB[0][:, 256 * dc:256 * dc + 256],
                                                   q_[0:rows, 1024 + h * 256 + dc * 128:1024 + h * 256 + dc * 128 + 128], vg[0:rows, :],
                                                   start=True, stop=True, skip_group_check=True)
                                return ins
                            tr.op('pe', f, reads=[('qk_sb', i), 'vg'], writes=[('pb', 0)])
                            tr.op('dve', lambda e, h=h, g128=g128: e.scalar_tensor_tensor(
                                S32[:, h, :, :], S32[:, h, :, :], g128, PB[0][:, :].rearrange("p (a b) -> p a b", b=256), ALU.mult, ALU.add),
                                reads=[('pb', 0), 'S32'], writes=['S32'])
                            tr.op('act', lambda e, h=h: e.copy(Sbf[:, h, :, :], S32[:, h, :, :]), reads=['S32'], writes=['Sb'])
                        else:
                            tr.op('act', lambda e, h=h: e.copy(qTzw[:, :, :, 60:64], qT[:, :, 0:64].rearrange("p a (b r) -> p a b r", r=4)),
                                  reads=['qT'], writes=['qTzw'])

                            def f(e, h=h, O=O, ob_=ob_):
                                e.matmul(PB[ob_][0:64, :], zero_b[:, 0:64], zero_b[:, :], start=True, stop=False, skip_group_check=True)
                                return e.matmul(O, sT[0:64, 0:64], v_[0:64, h * 256:h * 256 + 256], start=False, stop=False, skip_group_check=True)
                            tr.op('pe', f, reads=['sT', ('v_sb', i)] + CK, writes=[('pb', ob_)])
                            g4 = GAM[h] ** 4
                            def s0_load(bb, h=h):
                                j4 = bb % 4
                                tr.dma('sp', 'S0%d' % j4, S0[j4][:], sret[l, bb, h].rearrange("(dc p) e -> p dc e", p=128), writes=[('S0', j4)])
                            s0_load(0); s0_load(1)
                            for bb in range(16):
                                j = bb % 4
                                if bb + 2 < 16:
                                    s0_load(bb + 2)
                                tr.op('act', lambda e, j=j: e.copy(S0b[j][:], S0[j][:]), reads=[('S0', j)], writes=[('S0b', j)])
                                tr.op('act', lambda e, j=j, bb=bb, h=h: e.activation(vmk[j % 2][:, :], v_[0:64, h * 256:h * 256 + 256], AF.Copy,
                                                                               scale=mbm[:, bb:bb + 1]),
                                      reads=[('v_sb', i)] + CK, writes=[('vmk', j % 2)])

                                def f(e, j=j, bb=bb, O=O):
                                    for dc in range(2):
                                        ins = e.matmul(O, qTzw[:, dc, bb, 60 - 4 * bb:124 - 4 * bb], S0b[j][:, dc, :], start=False, stop=False,
                                                       skip_group_check=True)
                                    return ins
                                tr.op('pe', f, reads=['qTzw', ('S0b', j)], writes=[('pb', ob_)])

                                def f(e, j=j, h=h):
                                    for dc in range(2):
                                        ins = e.matmul(PB[j % 2][:, 256 * dc:256 * dc + 256],
                                                       q_[0:64, 1024 + h * 256 + dc * 128:1024 + h * 256 + dc * 128 + 128], vmk[j % 2][0:64, :],
                                                       start=True, stop=True, skip_group_check=True)
                                    return ins
                                tr.op('pe', f, reads=[('qk_sb', i), ('vmk', j % 2)], writes=[('pb', j % 2)])
                                tr.op('dve', lambda e, j=j: e.tensor_tensor(S0[j][:], S0[j][:], PB[j % 2][:, :].rearrange("p (a b) -> p a b", b=256), ALU.add),
                                      reads=[('pb', j % 2), ('S0', j)], writes=[('S0', j)])
                                tr.op('dve', lambda e, j=j, g4=g4: e.tensor_scalar(S0[j][:], S0[j][:], g4, 0.0, ALU.mult, ALU.add),
                                      reads=[('S0', j)], writes=[('S0', j)])
                                tr.dma('pool', 'S0%d' % j, o_rets[l, bb, h].rearrange("(dc p) e -> p dc e", p=128), S0[j][:],
                                       reads=[('S0', j)], writes=[('o_rets', l, bb, h)])
                            head_finish(rows, h, O, ('pb', ob_), i)
                    if not smp:
                        tr.dma('sp', 'ol%d' % i, OL[t * 128:(t + 1) * 128, :], olt[i][:, :], reads=[('olt', i)], writes=[('OL', t)])
                    else:
                        pooling(t, rows, i, True)
                        if dbg and l == dbg - 1:
                            tr.dma('sp', 'cat%d' % i, dbg_cat[t * 128:t * 128 + rows, :], cat[i][0:rows, :], reads=[('cat', i)], writes=[('dbgc', t)])
                        transpose_to(lambda c0, n, t=t, rows=rows: xt_dst(t, rows, c0, n), cat[i], rows, 16, [('cat', i)], xt_wk(t))

                tr.dma('sp', 'xs', ib[0:1024, :].rearrange("(h dc p) e -> p h dc e", p=128, dc=2), S32[:], reads=['S32'], writes=['ib0'])
                tr.dma('sp', 'xu', ib[1024:1088, :], Ud[NPT * 128 - 16:NPT * 128, :].rearrange("r (a e) -> (r a) e", a=4),
                       reads=uk(NPT - 1), writes=['ib1'])
                tr.collective('cc', lambda e: e.collective_compute("AllGather", ALU.bypass, replica_groups=[[2 * i_, 2 * i_ + 1] for i_ in range(ncores // 2)],
                                                                  ins=[ib_t.ap().opt()], outs=[ob_t.ap().opt()]),
                              reads=['ib0', 'ib1'], writes=['ob'])
                tr.dma('sp', 'xl', Sh32[:], ob[0:1024, :].rearrange("(h dc p) e -> p h dc e", p=128, dc=2), reads=['ob'], writes=['Sh32'])
                tr.dma('pool', 'xp', ubp[112:128, :], ob[1024:1088, :].rearrange("(r a) e -> r (a e)", a=4), reads=['ob', 'ubp'], writes=['ubp'])
                tr.op('act', lambda e: e.copy(Shb[:], Sh32[:]), reads=['Sh32'], writes=['Shb'])
                for h in range(4):
                    tr.op('dve', lambda e, h=h: e.scalar_tensor_tensor(S32[:, h, :, :], Sh32[:, h, :, :], csc[:, NPT, h:h + 1], S32[:, h, :, :],
                                                                       ALU.mult, ALU.add),
                          reads=['Sh32', 'S32'] + CK, writes=['S32'])
                tr.dma('sp', 'S32o', o_retp[l].rearrange("h (dc p) e -> p h dc e", p=128), S32[:], reads=['S32'], writes=[('o_retp', l)])

                def b_load(t):
                    i = t % 2
                    tr.dma('sp', 'qk%d' % i, qk_sb[i][:, 0:1024], QK[t * 128:(t + 1) * 128, 0:1024], reads=qkk(t), writes=[('qk_sb', i)])
                    tr.dma('sp', 'ol%d' % i, olt[i][:, :], OL[t * 128:(t + 1) * 128, :], reads=[('OL', t)], writes=[('olt', i)])
                    tr.dma('sp', 'g%d' % i, g_sb[i][:, :], Gd[t * 128:(t + 1) * 128, :], reads=gk(t), writes=[('g_sb', i)])
                    tr.dma('pool', 'ub%d' % (t % 3), ub[t % 3][:, :], Ud[t * 128:(t + 1) * 128, :], reads=uk(t), writes=[('ub', t % 3)])
                b_load(0)
                for t in range(NPT):
                    i = t % 2
                    if t + 1 < NPT:
                        b_load(t + 1)
                    q_ = qk_sb[i]

                    def f(e):
                        pq = bfv(PB[3])
                        for c in range(8):
                            ins = e.transpose(pq[:, c * 128:(c + 1) * 128], q_[:, c * 128:(c + 1) * 128], identb[:, :])
                        return ins
                    tr.op('pe', f, reads=[('qk_sb', i)] + CK, writes=[('pb', 3)])
                    tr.op('act', lambda e: e.copy(qT4[:, :, :], bfv(PB[3]).rearrange("p (a b) -> p a b", b=128)), reads=[('pb', 3)], writes=['qT4'])

                    def f(e):
                        for h in range(4):
                            for dc in range(2):
                                ins = e.matmul(PB[4 + h // 2][:, (h % 2) * 256:(h % 2) * 256 + 256], qT4[:, 2 * h + dc, :], Shb[:, h, dc, :],
                                               start=(dc == 0), stop=(dc == 1), skip_group_check=True)
                        return ins
                    tr.op('pe', f, reads=['qT4', 'Shb'], writes=[('pb', 4), ('pb', 5)])
                    s_ = st2
                    for h in range(4):
                        tr.op('dve', lambda e, h=h, t=t: e.scalar_tensor_tensor(osb4[:, h, :], OC4[:, h, :], csc[:, t, h:h + 1],
                                                                            olt[i][:, h * 256:h * 256 + 256], ALU.mult, ALU.add),
                              reads=[('pb', 4), ('pb', 5), ('olt', i)] + CK, writes=[('osb4', h)])
                        tr.op('act', lambda e, h=h: e.activation(junk[:, :], osb4[:, h, :], AF.Square, accum_out=s_[:, h:h + 1]),
                              reads=[('osb4', h)], writes=['junk', ('ss', h)])
                    tr.op('dve', lambda e: e.tensor_scalar(s_[:, 4:8], s_[:, 0:4], 1.0 / 256, EPS, ALU.mult, ALU.add),
                          reads=[('ss', h) for h in range(4)], writes=['ss2'])
                    tr.op('act', lambda e: e.activation(s_[:, 4:8], s_[:, 4:8], AF.Sqrt), reads=['ss2'], writes=['ss2'])
                    tr.op('dve', lambda e: e.reciprocal(s_[:, 4:8], s_[:, 4:8]), reads=['ss2'], writes=['ss2'])
                    for h in range(4):
                        tr.op('dve', lambda e, h=h: e.scalar_tensor_tensor(on4[:, h * 256:(h + 1) * 256], osb4[:, h, :], s_[:, 4 + h:5 + h],
                                                                       wrow2[:, h * 256:h * 256 + 256], ALU.mult, ALU.mult),
                              reads=[('osb4', h), 'ss2', 'wrow2'], writes=[('on4', h)])
                    tr.op('dve', lambda e: e.tensor_tensor(cat[i][:, 0:1024], on4[:, :], g_sb[i][:, :], ALU.mult),
                          reads=[('on4', h) for h in range(4)] + [('g_sb', i)], writes=[('cat', i)])
                    pooling(t, 128, i, False)
                    if dbg and l == dbg - 1:
                        tr.dma('sp', 'cat%d' % i, dbg_cat[t * 128:(t + 1) * 128, :], cat[i][:, :], reads=[('cat', i)], writes=[('dbgc', t)])
                    transpose_to(lambda c0, n, t=t: xt_dst(t, 128, c0, n), cat[i], 128, 16, [('cat', i)], xt_wk(t))
            if stop_after == 'S3':
                break

            linear_resid(w_out[l])
            if stop_after == 'S4':
                break

            with Stage() as S:
                MT = S([128, 16, 256], BF16)
                Vb = S([128, 2, D], BF16); KT = S([128, 16, 256], BF16)
                with Stage() as S2:
                    Kb = S2([128, 2, D], BF16)
                    norm_stage(mem, 2, lambda t: 128, normw[l * 4 + 3], lambda t, rows, c0, n: MT[:, c0:c0 + n, t * 128:t * 128 + rows],
                               lambda t: [('MT', t)], lambda t: [])

                    def mk_ep(dst, bcopy, nm):
                        def ep(t, rows, nb, b):
                            i = nxt('ev', 3)
                            tr.op('act', lambda e: e.copy(ev[i][0:rows, :], PB[b][0:rows, :]), reads=[('pb', b)], writes=[('ev', i)])
                            tr.op('dve', lambda e: e.tensor_copy(bcopy[:, t, nb * 512:(nb + 1) * 512], ev[i][0:rows, :]),
                                  reads=[('ev', i)], writes=[(nm, t)])
                            tr.dma('sp', 'ev%d' % i, dst[l][t * 128:t * 128 + rows, nb * 512:(nb + 1) * 512], ev[i][0:rows, :],
                                   reads=[('ev', i)], writes=[('omk', nm, l, t, nb)])
                        return ep
                    msrc = lambda t, rows, kc: MT[:, kc, t * 128:t * 128 + rows]
                    linear_tok(2, lambda t: 128, w_mk[l], D, mk_ep(o_mk, Kb, 'Kb'), src_fn=msrc, key='MT')
                    linear_tok(2, lambda t: 128, w_mv[l], D, mk_ep(o_mv, Vb, 'Vb'), src_fn=msrc, key='MT')
                    for mc in range(2):
                        transpose_to(lambda c0, n, mc=mc: KT[:, c0:c0 + n, mc * 128:(mc + 1) * 128], Kb[:, mc, :], 128, 16,
                                     [('Kb', mc)], [('KT', mc)])
                norm_stage(X, NT, trows, normw[l * 4 + 1], xt_dst, xt_wk, xkeys)

                def xq_ep(t, rows, nb, b):
                    i = nxt('ev', 3)
                    tr.op('act', lambda e: e.copy(evb[i][0:rows, :], PB[b][0:rows, :]), reads=[('pb', b)], writes=[('evb', i)])
                    tr.dma('sp', 'evb%d' % i, XQ[t * 128:t * 128 + rows, nb * 512:(nb + 1) * 512], evb[i][0:rows, :],
                           reads=[('evb', i)], writes=[('XQ', t, nb)])
                linear_tok(NT, trows, w_xq[l], D, xq_ep)

                qx = [S([128, D], BF16) for _ in range(2)]
                aqT2 = [S([128, 16, 128], BF16) for _ in range(2)]
                pe2 = [S([128, 4, 256]) for _ in range(2)]; pn2 = [S([128, 4 * 256], BF16) for _ in range(2)]
                pTs2 = [S([128, 8, 128], BF16) for _ in range(2)]
                st3 = [S([128, 16]) for _ in range(2)]
                attn = [S([128, D], BF16) for _ in range(2)]
                kc_b = [S([128, 2, D], BF16) for _ in range(2)]; vc_b = [S([128, 2, D], BF16) for _ in range(2)]
                KTs = [S([128, 16, 256], BF16) for _ in range(2)]
                sc = 512.0 ** -0.5
                SC4 = PSALL[:, 4:6, :].rearrange("p b (h m) -> p (b h) m", m=256)

                def attn_core(rows, ktile, kkeys, vtile, vkeys, maskcol, accumulate, aqT, aqk, z):
                    pe_ = pe2[z]; pn = pn2[z]; pTs = pTs2[z]
                    def f(e):
                        for h in range(4):
                            for dc in range(4):
                                ins = e.matmul(PB[4 + h // 2][0:rows, (h % 2) * 256:(h % 2) * 256 + 256], aqT[:, 4 * h + dc, 0:rows],
                                               ktile[:, 4 * h + dc, :], start=(dc == 0), stop=(dc == 3), skip_group_check=True)
                        return ins
                    tr.op('pe', f, reads=[aqk] + kkeys, writes=[('pb', 4), ('pb', 5)])
                    s_ = st3[z][:, 0:8]
                    tr.op('dve', lambda e: e.reduce_max(s_[0:rows, 0:4], SC4[0:rows], axis=AX.X), reads=[('pb', 4), ('pb', 5)], writes=[('st3', z)])
                    tr.op('dve', lambda e: e.tensor_scalar(s_[0:rows, 4:8], s_[0:rows, 0:4], -sc, 0.0, ALU.mult, ALU.add),
                          reads=[('st3', z)], writes=[('st3', z)])
                    s2 = st3[z][:, 8:16]
                    for h in range(4):
                        tr.op('act', lambda e, h=h: e.activation(pe_[0:rows, h, :], SC4[0:rows, h, :], AF.Exp, bias=s_[0:rows, 4 + h:5 + h], scale=sc,
                                                                 accum_out=s2[0:rows, h:h + 1]),
                              reads=[('pb', 4), ('pb', 5), ('st3', z)], writes=[('pe_', z, h), ('st0', z, h)])
                    tr.op('dve', lambda e: e.reciprocal(s2[0:rows, 4:8], s2[0:rows, 0:4]), reads=[('st0', z, h) for h in range(4)], writes=[('rinv', z)])
                    if maskcol is not None:
                        tr.op('dve', lambda e: e.tensor_scalar(s2[0:rows, 4:8], s2[0:rows, 4:8], maskcol, 0.0, ALU.mult, ALU.add),
                              reads=[('rinv', z)] + CK, writes=[('rinv', z)])
                    for h in range(4):
                        tr.op('dve', lambda e, h=h: e.tensor_scalar(pn[0:rows, h * 256:(h + 1) * 256], pe_[0:rows, h, :], s2[0:rows, 4 + h:5 + h], 0.0,
                                                                  ALU.mult, ALU.add),
                              reads=[('pe_', z, h), ('rinv', z)], writes=[('pn', z)])
                    transpose_to(lambda c0, n: pTs[:, c0:c0 + n, 0:rows], pn, rows, 8, [('pn', z)], [('pTs', z)])

                    def f(e):
                        for h in range(4):
                            for mc in range(2):
                                if accumulate:
                                    ins = e.matmul(PB[h][0:rows, :], pTs[:, 2 * h + mc, 0:rows], vtile[:, mc, h * 512:(h + 1) * 512],
                                                   start=False, stop=False, skip_group_check=True)
                                else:
                                    ins = e.matmul(PB[h][0:rows, :], pTs[:, 2 * h + mc, 0:rows], vtile[:, mc, h * 512:(h + 1) * 512],
                                                   start=(mc == 0), stop=(mc == 1))
                        return ins
                    tr.op('pe', f, reads=[('pTs', z)] + vkeys, writes=[('pb', h) for h in range(4)])

                xqk = lambda t: [('XQ', t, nb) for nb in range(4)]

                def at_load(t):
                    rows = trows(t)
                    tr.dma('sp', 'qx%d' % (t % 2), qx[t % 2][0:rows, :], XQ[t * 128:t * 128 + rows, :], reads=xqk(t), writes=[('qx', t % 2)])

                def kv_load(bb):
                    j = bb % 2
                    tr.dma('pool', 'kcb%d' % j, kc_b[j][:], ck[l, bb].rearrange("(mc p) n -> p mc n", p=128), writes=[('kcb', j, 0), ('kcb', j, 1)])
                    tr.dma('pool', 'vcb%d' % j, vc_b[j][:], cv[l, bb].rearrange("(mc p) n -> p mc n", p=128), writes=[('vcb', j)])
                at_load(0)
                for t in range(NT):
                    rows = trows(t); i = t % 2
                    if t + 1 < NT:
                        at_load(t + 1)
                    if t == NPT - 1:
                        kv_load(0)
                    aqT = aqT2[i]
                    transpose_to(lambda c0, n, aqT=aqT: aqT[:, c0:c0 + n, 0:rows], qx[i], rows, 16, [('qx', i)], [('aqT', i)])
                    if t < NPT:
                        attn_core(rows, KT, [('KT', 0), ('KT', 1)], Vb, [('Vb', 0), ('Vb', 1)], None, False, aqT, ('aqT', i), i)
                    else:
                        def f(e):
                            for h in range(4):
                                ins = e.matmul(PB[h][0:64, :], zero_b[:, 0:64], zero_b[:, :], start=True, stop=False, skip_group_check=True)
                            return ins
                        tr.op('pe', f, reads=CK, writes=[('pb', h) for h in range(4)])
                        for bb in range(16):
                            j = bb % 2
                            if bb + 1 < 16:
                                kv_load(bb + 1)
                            for mc in range(2):
                                transpose_to(lambda c0, n, mc=mc, j=j: KTs[j][:, c0:c0 + n, mc * 128:(mc + 1) * 128], kc_b[j][:, mc, :], 128, 16,
                                             [('kcb', j, mc)], [('KTs', j, mc)])
                            attn_core(64, KTs[j], [('KTs', j, 0), ('KTs', j, 1)], vc_b[j], [('vcb', j)], mbm[:, bb:bb + 1], True, aqT, ('aqT', i), j)
                    tr.op('act', lambda e, rows=rows, i=i: e.copy(attn[i][0:rows, :], PSALL[0:rows, 0:4, :].rearrange("p b n -> p (b n)")),
                          reads=[('pb', h) for h in range(4)], writes=[('attn', i)])
                    transpose_to(lambda c0, n, t=t, rows=rows: xt_dst(t, rows, c0, n), attn[i], rows, 16, [('attn', i)], xt_wk(t))
            linear_resid(w_xo[l])
            if stop_after == 'S9':
                break

            norm_stage(X, NT, trows, normw[l * 4 + 2], xt_dst, xt_wk, xkeys)
            with Stage() as S:
                XT2 = S([128, 16, TOK], BF16)
                rq = [S([128, 512]) for _ in range(2)]
                for kq in range(4):
                    for nbq in range(4):
                        nb = kq * 4 + nbq
                        s = wload(w_up[l][:, nb * 512:(nb + 1) * 512], 16)
                        for mi in range(4):
                            fc = nbq * 4 + mi
                            for (c0, w) in TG:
                                b = nxt('pb', 4)
                                tl = list(range(c0 // 128, (c0 + w + 127) // 128))

                                def f(e, s=s, mi=mi, c0=c0, w=w, b=b):
                                    for kc in range(16):
                                        ins = e.matmul(PB[b][:, 0:w], wbuf[:, s, kc, mi * 128:(mi + 1) * 128], XT[:, kc, c0:c0 + w],
                                                       start=(kc == 0), stop=(kc == 15))
                                    return ins
                                tr.op('pe', f, reads=[('XT', t) for t in tl] + [('w', s)], writes=[('pb', b)])
                                i = nxt('rq', 2)
                                tr.op('act', lambda e, i=i, b=b, w=w: e.activation(rq[i][:, 0:w], PB[b][:, 0:w], AF.Relu),
                                      reads=[('pb', b)], writes=[('rq', i)])
                                tr.op('dve', lambda e, i=i, w=w, fc=fc, c0=c0: e.tensor_tensor(XT2[:, fc, c0:c0 + w], rq[i][:, 0:w], rq[i][:, 0:w], ALU.mult),
                                      reads=[('rq', i)], writes=[('XT2', t) for t in tl])
                    linear_tok(NT, trows, w_down[l][kq * 2048:(kq + 1) * 2048, :], D, resid_epilogue, prefetch=resid_prefetch,
                               src_fn=lambda t, rows, kc: XT2[:, kc, t * 128:t * 128 + rows], key='XT2')

        if dbg:
            tr.dma('sp', 'dbgx', dbg_x, X, reads=[k for t in range(NT) for k in xkeys(t)], writes=['dbgx'])
        if stop_after is None:
            norm_stage(X, NT, trows, normw[8], None, None, xkeys,
                       out_dram=lambda t, rows: (y_p[t * 128:t * 128 + rows, :] if t < NPT else y_s[:, :]))
        tr.finish()
    return nc


def make_inputs(c, A, consts):
    b, half = c // 2, c % 2
    m = {
        "xp": A["x_prompt"][b, half * NPT * 128:(half + 1) * NPT * 128], "xs": A["x_sample"][16 * c:16 * c + 16].reshape(64, D),
        "mem": A["mem_prompt"][b],
        "sret": A["state_ret"][:, 16 * c:16 * c + 16], "spool": A["state_pool"][:, 16 * c:16 * c + 16],
        "ck": A["cache_mem_k"][:, 16 * c:16 * c + 16].reshape(2, 16, 256, D),
        "cv": A["cache_mem_v"][:, 16 * c:16 * c + 16].reshape(2, 16, 256, D),
        "w_in": A["w_in"], "pool_w": A["pool_w"], "w_out": A["w_out"], "w_xq": A["w_xq"], "w_mk": A["w_mk"],
        "w_mv": A["w_mv"], "w_xo": A["w_xo"], "w_up": A["w_up"], "w_down": A["w_down"],
        "normw": A["normw"], "retw": A["ret_norm_w"], "pscale": A["pool_scale"],
    }
    m.update(consts[half])
    return {k: np.ascontiguousarray(v) for k, v in m.items()}


def kernel(**inputs):
    A = {k: np.asarray(v, dtype=np.float32) for k, v in inputs.items()}
    rows = []
    for l in range(2):
        rows += [A["attn_norm_w"][l], A["xattn_norm_w"][l], A["mlp_norm_w"][l], A["mem_norm_w"][l]]
    rows.append(A["final_norm_w"])
    A["normw"] = np.stack(rows)
    consts = [host_consts(NPT, half * NPT * 128, 16384, half) for half in range(2)]
    nc = build()
    in_maps = [make_inputs(c, A, consts) for c in range(8)]
    res = run_bass_kernel_spmd(nc, in_maps, core_ids=list(range(8)))
    R = res.results
    y_p = np.stack([np.concatenate([R[2 * b]["y_p"], R[2 * b + 1]["y_p"]], axis=0) for b in range(4)])
    y_s = np.concatenate([R[c]["y_s"].reshape(16, 4, D) for c in range(8)], axis=0)
    retp = np.stack([R[2 * b + 1]["o_retp"] for b in range(4)], axis=1)
    bufp = np.stack([R[2 * b + 1]["o_bufp"] for b in range(4)], axis=1)
    mk = np.stack([R[2 * b]["o_mk"].reshape(2, 256, 4, 512) for b in range(4)], axis=1)
    mv = np.stack([R[2 * b]["o_mv"].reshape(2, 256, 4, 512) for b in range(4)], axis=1)
    rets = np.concatenate([R[c]["o_rets"] for c in range(8)], axis=1)
    bufs = np.concatenate([R[c]["o_bufs"] for c in range(8)], axis=1)
    f = lambda a: np.ascontiguousarray(a, dtype=np.float32)
    return (f(y_p), f(y_s), f(retp), f(bufp), f(mk), f(mv), f(rets), f(bufs))
```
